# Optimizing a Trainium2 kernel written in Bass

```python
import math
import jax, jax.numpy as jnp
from jax import lax
import numpy as np

D_MODEL = 1024
BATCH = 8
SEQ = 2048
DEPTH = 4

MEM_LEN = 256
D_FF = 2816
CONV_D = D_MODEL
CONV_K = 31
N_HEADS = 8
HEAD_DIM = 64
V_DIM = 2 * HEAD_DIM
QK_D = N_HEADS * 2 * HEAD_DIM
ATTN_D = N_HEADS * V_DIM
Q_BLOCK = 128
N_BUCKETS = 32
MAX_DISTANCE = 128
POOL_WINDOWS = (2, 4, 8, 16)
POOL_GROUPS = len(POOL_WINDOWS)
POOL_D = D_MODEL
POOL_GD = POOL_D // POOL_GROUPS
X_HEADS = 4
X_HEAD_DIM = D_MODEL // X_HEADS
X_D = X_HEADS * X_HEAD_DIM
N_BRANCH = 3
SPLITS = (2 * CONV_D,
          2 * CONV_D + QK_D,
          2 * CONV_D + 2 * QK_D,
          2 * CONV_D + 2 * QK_D + ATTN_D,
          2 * CONV_D + 2 * QK_D + ATTN_D + POOL_D)
IN_COLS = SPLITS[-1] + N_BRANCH * D_MODEL
RMS_EPS = 1e-6
LN_EPS = 1e-5
NEG_INF = -1e30

kernel_name = "hybrid_conv_diffattn_pool_macaron_trunk"


def rms_norm(x, g):
    xf = x.astype(jnp.float32)
    y = xf * lax.rsqrt(jnp.mean(xf * xf, axis=-1, keepdims=True) + RMS_EPS)
    return (y * g.astype(jnp.float32)).astype(x.dtype)


def layer_norm(x, g, b):
    xf = x.astype(jnp.float32)
    mu = jnp.mean(xf, axis=-1, keepdims=True)
    xc = xf - mu
    y = xc * lax.rsqrt(jnp.mean(xc * xc, axis=-1, keepdims=True) + LN_EPS)
    return (y * g.astype(jnp.float32) + b.astype(jnp.float32)).astype(x.dtype)


def swiglu(h, w1, w3, w2):
    return (jax.nn.silu(h @ w1) * (h @ w3)) @ w2


def t5_bucket(n):
    n = jnp.maximum(n, 0)
    max_exact = N_BUCKETS // 2
    nf = jnp.maximum(n, 1).astype(jnp.float32)
    large = max_exact + (jnp.log(nf / max_exact) / math.log(MAX_DISTANCE / max_exact)
                         * (N_BUCKETS - max_exact)).astype(jnp.int32)
    large = jnp.minimum(large, N_BUCKETS - 1)
    return jnp.where(n < max_exact, n, large)


def conv_branch(u, w_dw, b_dw, ln_g, ln_b, w_pw):
    a, gt = jnp.split(u, 2, axis=-1)
    z = a * jax.nn.sigmoid(gt)
    z = lax.conv_general_dilated(z, w_dw[:, None, :], window_strides=(1,),
                                 padding=((CONV_K - 1, 0),),
                                 dimension_numbers=('NWC', 'WIO', 'NWC'),
                                 feature_group_count=CONV_D) + b_dw
    z = jax.nn.silu(layer_norm(z, ln_g, ln_b))
    return z @ w_pw


def diff_attention(q, k, v, lam, lam_init, subln_g, rel_bias):
    B, S = q.shape[0], q.shape[1]
    lamf = lam.astype(jnp.float32)
    lam_full = (jnp.exp(jnp.sum(lamf[0] * lamf[1])) - jnp.exp(jnp.sum(lamf[2] * lamf[3])) + lam_init)
    scale = HEAD_DIM ** -0.5
    n_blocks = S // Q_BLOCK
    k_pos = jnp.arange(S)
    qb = q.reshape(B, n_blocks, Q_BLOCK, N_HEADS, 2, HEAD_DIM).transpose(1, 0, 2, 3, 4, 5)

    def block(args):
        i, qi = args
        q_pos = i * Q_BLOCK + jnp.arange(Q_BLOCK)
        rel = q_pos[:, None] - k_pos[None, :]
        bias = rel_bias.astype(jnp.float32)[t5_bucket(rel)]
        logits = jnp.einsum('bqhmd,bkhmd->bhmqk', qi, k).astype(jnp.float32) * scale
        logits = logits + bias.transpose(2, 0, 1)[None, :, None]
        logits = jnp.where((rel >= 0)[None, None, None], logits, NEG_INF)
        p = jax.nn.softmax(logits, axis=-1)
        p = p[:, :, 0] - lam_full * p[:, :, 1]
        return jnp.einsum('bhqk,bkhv->bqhv', p.astype(v.dtype), v)

    out = lax.map(block, (jnp.arange(n_blocks), qb))
    out = out.transpose(1, 0, 2, 3, 4).reshape(B, S, N_HEADS, V_DIM)
    out = rms_norm(out, subln_g) * (1.0 - lam_init)
    return out.reshape(B, S, ATTN_D)


def pool_branch(u, w_grp, scale, w_o):
    B, S = u.shape[0], u.shape[1]
    uf = u.astype(jnp.float32).reshape(B, S, POOL_GROUPS, POOL_GD)
    cz = jnp.pad(jnp.cumsum(uf, axis=1), ((0, 0), (1, 0), (0, 0), (0, 0)))
    t = jnp.arange(S)
    outs = []
    for g, w in enumerate(POOL_WINDOWS):
        c = cz[:, :, g]
        prev = jnp.pad(c, ((0, 0), (w - 1, 0), (0, 0)))[:, :S]
        cnt = jnp.minimum(t + 1, w).astype(jnp.float32)[None, :, None]
        outs.append((c[:, 1:] - prev) / cnt)
    pooled = (jnp.stack(outs, axis=2) - uf).astype(u.dtype)
    y = jnp.einsum('bsgc,gcd->bsgd', pooled, w_grp).reshape(B, S, POOL_D) * scale
    return y @ w_o


def cross_attention(h, mem_n, wq, wkv, wo):
    B, S = h.shape[0], h.shape[1]
    M = mem_n.shape[1]
    q = (h @ wq).reshape(B, S, X_HEADS, X_HEAD_DIM)
    k, v = jnp.split(mem_n @ wkv, 2, axis=-1)
    k = k.reshape(B, M, X_HEADS, X_HEAD_DIM)
    v = v.reshape(B, M, X_HEADS, X_HEAD_DIM)
    logits = jnp.einsum('bshd,bmhd->bhsm', q, k).astype(jnp.float32) * (X_HEAD_DIM ** -0.5)
    p = jax.nn.softmax(logits, axis=-1)
    o = jnp.einsum('bhsm,bmhd->bshd', p.astype(v.dtype), v).reshape(B, S, X_D)
    return o @ wo


def mixer_block(h, w_in, conv_dw, conv_dw_b, conv_ln_g, conv_ln_b, conv_pw,
                lam, lam_init, subln_g, attn_o, rel_bias, pool_w, pool_scale, pool_o, w_out):
    B, S = h.shape[0], h.shape[1]
    u = h @ w_in
    u_conv, q, k, v, u_pool, gates = jnp.split(u, SPLITS, axis=-1)
    y_conv = conv_branch(u_conv, conv_dw, conv_dw_b, conv_ln_g, conv_ln_b, conv_pw)
    q = q.reshape(B, S, N_HEADS, 2, HEAD_DIM)
    k = k.reshape(B, S, N_HEADS, 2, HEAD_DIM)
    v = v.reshape(B, S, N_HEADS, V_DIM)
    y_attn = diff_attention(q, k, v, lam, lam_init, subln_g, rel_bias) @ attn_o
    y_pool = pool_branch(u_pool, pool_w, pool_scale, pool_o)
    g = jax.nn.sigmoid(gates.reshape(B, S, N_BRANCH, D_MODEL))
    merged = g[:, :, 0] * y_conv + g[:, :, 1] * y_attn + g[:, :, 2] * y_pool
    return merged @ w_out


def setup_inputs(seed: int = 0) -> dict:
    key = jax.random.key(seed)
    ks = jax.random.split(key, 24)
    f32 = jnp.float32

    def nrm(k, shape, scale):
        return jax.random.normal(k, shape, f32) * scale

    L = DEPTH
    return {
        "x": nrm(ks[0], (BATCH, SEQ, D_MODEL), 1.0),
        "mem": nrm(ks[1], (BATCH, MEM_LEN, D_MODEL), 1.0),
        "norm_g": 1.0 + nrm(ks[2], (L, 8, D_MODEL), 0.05),
        "ffn_w1": nrm(ks[3], (L, 2, D_MODEL, D_FF), D_MODEL ** -0.5),
        "ffn_w3": nrm(ks[4], (L, 2, D_MODEL, D_FF), D_MODEL ** -0.5),
        "ffn_w2": nrm(ks[5], (L, 2, D_FF, D_MODEL), D_FF ** -0.5),
        "w_in": nrm(ks[6], (L, D_MODEL, IN_COLS), D_MODEL ** -0.5),
        "conv_dw": nrm(ks[7], (L, CONV_K, CONV_D), CONV_K ** -0.5),
        "conv_dw_b": nrm(ks[8], (L, CONV_D), 0.02),
        "conv_ln_g": 1.0 + nrm(ks[9], (L, CONV_D), 0.05),
        "conv_ln_b": nrm(ks[10], (L, CONV_D), 0.02),
        "conv_pw": nrm(ks[11], (L, CONV_D, D_MODEL), CONV_D ** -0.5),
        "attn_lam": nrm(ks[12], (L, 4, HEAD_DIM), 0.1),
        "attn_subln": 1.0 + nrm(ks[13], (L, V_DIM), 0.05),
        "attn_o": nrm(ks[14], (L, ATTN_D, D_MODEL), ATTN_D ** -0.5),
        "rel_bias": nrm(ks[15], (N_BUCKETS, N_HEADS), 0.5),
        "pool_w": nrm(ks[16], (L, POOL_GROUPS, POOL_GD, POOL_GD), POOL_GD ** -0.5),
        "pool_scale": 1.0 + nrm(ks[17], (L, POOL_D), 0.1),
        "pool_o": nrm(ks[18], (L, POOL_D, D_MODEL), POOL_D ** -0.5),
        "w_out": nrm(ks[19], (L, D_MODEL, D_MODEL), D_MODEL ** -0.5),
        "mem_norm": 1.0 + nrm(ks[20], (L, D_MODEL), 0.05),
        "xattn_q": nrm(ks[21], (L, D_MODEL, X_D), D_MODEL ** -0.5),
        "xattn_kv": nrm(ks[22], (L, D_MODEL, 2 * X_D), D_MODEL ** -0.5),
        "xattn_o": nrm(ks[23], (L, X_D, D_MODEL), X_D ** -0.5),
    }


def reference(x, mem, norm_g, ffn_w1, ffn_w3, ffn_w2, w_in, conv_dw, conv_dw_b, conv_ln_g, conv_ln_b,
              conv_pw, attn_lam, attn_subln, attn_o, rel_bias, pool_w, pool_scale, pool_o, w_out,
              mem_norm, xattn_q, xattn_kv, xattn_o):
    for l in range(DEPTH):
        ng = norm_g[l]
        lam_init = 0.8 - 0.6 * math.exp(-0.3 * l)
        h = rms_norm(x, ng[0])
        x = x + 0.5 * rms_norm(swiglu(h, ffn_w1[l, 0], ffn_w3[l, 0], ffn_w2[l, 0]), ng[1])
        h = rms_norm(x, ng[2])
        y = mixer_block(h, w_in[l], conv_dw[l], conv_dw_b[l], conv_ln_g[l], conv_ln_b[l], conv_pw[l],
                        attn_lam[l], lam_init, attn_subln[l], attn_o[l], rel_bias,
                        pool_w[l], pool_scale[l], pool_o[l], w_out[l])
        x = x + rms_norm(y, ng[3])
        h = rms_norm(x, ng[4])
        y = cross_attention(h, rms_norm(mem, mem_norm[l]), xattn_q[l], xattn_kv[l], xattn_o[l])
        x = x + rms_norm(y, ng[5])
        h = rms_norm(x, ng[6])
        x = x + 0.5 * rms_norm(swiglu(h, ffn_w1[l, 1], ffn_w3[l, 1], ffn_w2[l, 1]), ng[7])
    return x
```

```python
import math
from contextlib import ExitStack, contextmanager

import numpy as np
import concourse.bass as bass
import concourse.mybir as mybir
from concourse.bass_utils import run_bass_kernel_spmd

F32 = mybir.dt.float32
BF16 = mybir.dt.bfloat16
AF = mybir.ActivationFunctionType
ALU = mybir.AluOpType

D = 1024
S = 2048
L = 4
DFF = 2816
KD = 8
KF = 22
MEM = 256
NH = 8
XH = 4
CK = 31
IN_COLS = 9216
RMS_EPS = 1e-6
LN_EPS = 1e-5
NEG = -30000.0
FAST_RECIP = False

V_NG = 0
V_CDW = 64
V_CDB = V_CDW + 8 * CK
V_LNG = V_CDB + 8
V_LNB = V_LNG + 8
V_PSC = V_LNB + 8
V_MN = V_PSC + 8
V_SUB = V_MN + 8
V_PER = V_SUB + 1


class Stream:
    def __init__(self, name, sem, step):
        self.name = name
        self.sem = sem
        self.step = step
        self.count = 0


class Engine(Stream):
    def __init__(self, name, sem, handle):
        super().__init__(name, sem, 1)
        self.h = handle
        self.seen = {}
        self.ninst = 0


class Buf:
    __slots__ = ("w", "r")

    def __init__(self):
        self.w = None
        self.r = {}


class FW:
    def __init__(self, nc, ctx):
        self.nc = nc

        def sem(n):
            return ctx.enter_context(nc.semaphore(n))

        self.pe = Engine("pe", sem("s_pe"), nc.tensor)
        self.act = Engine("act", sem("s_act"), nc.scalar)
        self.dve = Engine("dve", sem("s_dve"), nc.vector)
        self.pool = Engine("pool", sem("s_pool"), nc.gpsimd)
        self.sp = Engine("sp", sem("s_sp"), nc.sync)
        self.engines = [self.pe, self.act, self.dve, self.pool, self.sp]
        self.dq = {}
        self.dqi = {}
        for n, k in (("w", 8), ("a", 8), ("o", 1)):
            self.dq[n] = [Stream(f"dq_{n}{i}", sem(f"s_dq_{n}{i}"), 16) for i in range(k)]
            self.dqi[n] = 0

    def _wait(self, eng, reads, writes):
        need = {}
        for b in reads:
            if b.w is not None:
                st, c = b.w
                if need.get(st, 0) < c:
                    need[st] = c
        own_ok = eng is not self.pe
        for b in writes:
            if b.w is not None:
                st, c = b.w
                if (st is not eng or own_ok) and need.get(st, 0) < c:
                    need[st] = c
            for st, c in b.r.items():
                if (st is not eng or own_ok) and need.get(st, 0) < c:
                    need[st] = c
        for st, c in need.items():
            if eng.seen.get(st, 0) < c:
                eng.h.wait_ge(st.sem, c)
                eng.seen[st] = c

    def _mark(self, st, cid, reads, writes):
        for b in reads:
            if b.r.get(st, 0) < cid:
                b.r[st] = cid
        for b in writes:
            b.w = (st, cid)
            b.r = {}

    def op(self, eng, ins_fn, reads=(), writes=()):
        self._wait(eng, reads, writes)
        ins = ins_fn(eng.h)
        eng.count += 1
        ins.then_inc(eng.sem, 1)
        eng.ninst += 1
        self._mark(eng, eng.count, reads, writes)
        return ins

    def mm_group(self, mms, reads, writes):
        eng = self.pe
        self._wait(eng, reads, writes)
        n = len(mms)
        ins = None
        for i, (o, l, r) in enumerate(mms):
            ins = eng.h.matmul(o, l, r, start=(i == 0), stop=(i == n - 1))
            eng.ninst += 1
        eng.count += 1
        ins.then_inc(eng.sem, 1)
        self._mark(eng, eng.count, reads, writes)

    def mm_multi(self, mms, reads, writes):
        eng = self.pe
        self._wait(eng, reads, writes)
        ins = None
        for (o, l, r, st_, sp_) in mms:
            ins = eng.h.matmul(o, l, r, start=st_, stop=sp_)
            eng.ninst += 1
        eng.count += 1
        ins.then_inc(eng.sem, 1)
        self._mark(eng, eng.count, reads, writes)

    def mm(self, o, l, r, start, stop, reads, writes):
        eng = self.pe
        self._wait(eng, reads, writes)
        ins = eng.h.matmul(o, l, r, start=start, stop=stop)
        eng.ninst += 1
        eng.count += 1
        ins.then_inc(eng.sem, 1)
        self._mark(eng, eng.count, reads, writes)

    def dma(self, qeng, dq, out, in_, reads=(), writes=()):
        ring = self.dq[dq]
        st = ring[self.dqi[dq] % len(ring)]
        self.dqi[dq] += 1
        if len(ring) > 1 and st.count > 0 and qeng.seen.get(st, 0) < st.count:
            qeng.h.wait_ge(st.sem, st.count)
            qeng.seen[st] = st.count
        self._wait(qeng, reads, writes)
        ins = qeng.h.dma_start(out=out, in_=in_)
        st.count += 16
        ins.then_inc(st.sem, 16)
        qeng.ninst += 1
        self._mark(st, st.count, reads, writes)
        return ins

    def barrier(self):
        streams = list(self.engines) + [s for r in self.dq.values() for s in r]
        for e in self.engines:
            for st in streams:
                if st is e:
                    continue
                if st.count > 0 and e.seen.get(st, 0) < st.count:
                    e.h.wait_ge(st.sem, st.count)
                    e.seen[st] = st.count


class TPool:
    def __init__(self, kb, name, shape, dtype, n):
        self.tiles = [kb.sb(name, shape, dtype) for _ in range(n)]
        self.bufs = [Buf() for _ in range(n)]
        self.i = 0

    def next(self):
        j = self.i % len(self.tiles)
        self.i += 1
        return self.tiles[j], self.bufs[j]


class KB:
    def __init__(self, nc, depth, plan=None):
        self.nc = nc
        self.depth = depth
        self.plan = plan
        self.uid = 0

    def sb(self, name, shape, dtype):
        self.uid += 1
        return self.cur.enter_context(self.nc.sbuf_tensor(f"{name}_{self.uid}", shape, dtype))

    @contextmanager
    def scope(self):
        prev = self.cur
        with ExitStack() as st:
            self.cur = st
            yield
            self.fw.barrier()
        self.cur = prev

    def psum(self, banks=None):
        banks = banks or self.all_banks
        j = banks[self.ps_i % len(banks)]
        self.ps_i += 1
        return self.ps[j], self.psb[j]

    def din(self, name, shape):
        return self.nc.dram_tensor(name, list(shape), F32, kind="ExternalInput").ap()

    def build(self):
        nc = self.nc
        depth = self.depth
        self.xT_d = self.din("xT", [D, S])
        self.memT_d = self.din("memT", [D, MEM])
        self.vecs_d = self.din("vecs", [128, L * V_PER])
        self.lamb_d = self.din("lamb", [128, L * 256])
        self.bstrip_d = self.din("bstrip", [128, NH * 256])
        self.b31_d = self.din("b31b", [128, NH])
        self.ident_d = self.din("ident", [128, 128])
        self.poolc_d = self.din("poolc", [128, 64])
        self.w1_d = self.din("ffn_w1", [L, 2, D, DFF])
        self.w3_d = self.din("ffn_w3", [L, 2, D, DFF])
        self.w2_d = self.din("ffn_w2", [L, 2, DFF, D])
        self.win_d = self.din("w_in", [L, D, IN_COLS])
        self.cpw_d = self.din("conv_pw", [L, D, D])
        self.ao_d = self.din("attn_o", [L, D, D])
        self.pw_d = self.din("pool_w", [L, 4, 256, 256])
        self.po_d = self.din("pool_o", [L, D, D])
        self.wo_d = self.din("w_out", [L, D, D])
        self.xq_d = self.din("xattn_q", [L, D, D])
        self.xkv_d = self.din("xattn_kv", [L, D, 2 * D])
        self.xo_d = self.din("xattn_o", [L, D, D])
        self.out_d = nc.dram_tensor("outT", [D, S], F32, kind="ExternalOutput").ap()

        with ExitStack() as ctx:
            self.cur = ctx
            self.fw = fw = FW(nc, ctx)
            self.ps = [ctx.enter_context(nc.psum_tensor(f"ps{i}", [128, 512], F32)) for i in range(8)]
            self.psb = [Buf() for _ in range(8)]
            self.all_banks = list(range(8))
            self.ps_i = 0
            self.x = self.sb("x", [128, KD, S], F32)
            self.xb = [[Buf() for _ in range(4)] for _ in range(KD)]
            self.vecs = self.sb("vecs", [128, V_PER], F32)
            self.vecs_b = Buf()
            self.gh = self.sb("gh", [128, 64], F32)
            self.ones1f = self.sb("ones1f", [128, 128], F32)
            self.rx = self.sb("rx", [128, S], F32)
            self.rxb = [Buf() for _ in range(4)]
            self.cur_layer = -1
            self.gh_b = Buf()
            self.onesD = self.sb("onesD", [128, 128], BF16)
            self.ones128 = self.sb("ones128", [128, 128], BF16)
            self.ones1 = self.sb("ones1", [128, 128], BF16)
            self.identf = self.sb("identf", [128, 128], F32)
            self.identb = self.sb("identb", [128, 128], BF16)
            self.cb = Buf()
            self.epsr = self.sb("epsr", [128, 1], F32)
            self.epsl = self.sb("epsl", [128, 1], F32)

            fw.dma(fw.sp, "a", self.identf[:], self.ident_d, writes=[self.cb])
            xv = self.xT_d.rearrange("(c p) t -> p c t", p=128)
            for c in range(KD):
                fw.dma(fw.sp, "a", self.x[:, c, :], xv[:, c, :], writes=self.xb[c])
            fw.op(fw.dve, lambda h: h.memset(self.onesD[:], 1.0 / D), writes=[self.cb])
            fw.op(fw.dve, lambda h: h.memset(self.ones128[:], 1.0 / 128), writes=[self.cb])
            fw.op(fw.dve, lambda h: h.memset(self.ones1[:], 1.0), writes=[self.cb])
            fw.op(fw.dve, lambda h: h.memset(self.epsr[:], RMS_EPS), writes=[self.cb])
            fw.op(fw.dve, lambda h: h.memset(self.epsl[:], LN_EPS), writes=[self.cb])
            fw.op(fw.dve, lambda h: h.tensor_copy(self.identb[:], self.identf[:]), reads=[self.cb], writes=[self.cb])
            fw.op(fw.dve, lambda h: h.memset(self.ones1f[:], 1.0), writes=[self.cb])
            fw.barrier()

            with self.scope():
                sqpool = TPool(self, "sq", [128, 512], BF16, 3)
                rtpool = TPool(self, "rt", [128, 512], F32, 2)
                for tb in range(4):
                    self.update_rstd(tb, sqpool, rtpool)

            plan = self.plan or [(l, ph) for l in range(depth) for ph in ("ffn0", "mixer", "xattn", "ffn1")]
            for (l, ph) in plan:
                if l != self.cur_layer:
                    self.cur_layer = l
                    fw.barrier()
                    fw.dma(fw.sp, "a", self.vecs[:], self.vecs_d[:, l * V_PER:(l + 1) * V_PER], writes=[self.vecs_b])
                    fw.op(fw.dve, lambda h: h.tensor_scalar(self.gh[:], self.vecs[:, V_NG:V_NG + 64], 0.5, None, ALU.mult),
                          reads=[self.vecs_b], writes=[self.gh_b])
                getattr(self, "ph_" + ph)(l)

            fw.barrier()
            ov = self.out_d.rearrange("(c p) t -> p c t", p=128)
            for c in range(KD):
                fw.dma(fw.sp, "o", ov[:, c, :], self.x[:, c, :], reads=self.xb[c])
            st = fw.dq["o"][0]
            fw.sp.h.wait_ge(st.sem, st.count)
            self.stats = {e.name: (e.ninst, e.count) for e in fw.engines}

    def gcol(self, l, n, c):
        o = V_NG + n * 8 + c
        return self.vecs[:, o:o + 1]

    def ghcol(self, l, n, c):
        o = n * 8 + c
        return self.gh[:, o:o + 1]

    def vcol(self, l, off, c):
        o = off + c
        return self.vecs[:, o:o + 1]

    def rstd_blocks(self, srcs, nblk, rstd, rstd_b, ones, eps_tile, sqpool, rtpool, W=512, sq_on_dve=False, banks=None):
        fw = self.fw
        n = len(srcs)
        for tb in range(nblk):
            ps, pb = self.psum(banks)
            for c, (apf, bf) in enumerate(srcs):
                sq, sqb = sqpool.next()
                if sq_on_dve:
                    fw.op(fw.dve, lambda h: h.tensor_tensor(sq[:, :W], apf(tb), apf(tb), ALU.mult), reads=bf(tb), writes=[sqb])
                else:
                    fw.op(fw.act, lambda h: h.activation(sq[:, :W], apf(tb), AF.Square), reads=bf(tb), writes=[sqb])
                fw.mm(ps[:, :W], ones, sq[:, :W], c == 0, c == n - 1, reads=[sqb, self.cb], writes=[pb])
            rt, rtb = rtpool.next()
            fw.op(fw.act, lambda h: h.activation(rt[:, :W], ps[:, :W], AF.Ln, bias=eps_tile, scale=1.0),
                  reads=[pb, self.cb], writes=[rtb])
            fw.op(fw.act, lambda h: h.activation(rstd[:, tb * W:(tb + 1) * W], rt[:, :W], AF.Exp, scale=-0.5),
                  reads=[rtb], writes=[rstd_b])

    def update_rstd(self, tb, sqpool, rtpool, sq_on_dve=False, banks=None):
        ts = slice(tb * 512, (tb + 1) * 512)
        srcs = [(lambda t_, c=c: self.x[:, c, ts], lambda t_, c=c: [self.xb[c][tb]]) for c in range(KD)]
        self.rstd_blocks(srcs, 1, self.rx[:, ts], self.rxb[tb], self.onesD[:], self.epsr[:], sqpool, rtpool,
                         sq_on_dve=sq_on_dve, banks=banks)

    def recip(self, out, in_, reads, writes):
        if FAST_RECIP:
            self.fw.op(self.fw.dve, lambda h: h.reciprocal_approx_fast(out, in_), reads=reads, writes=writes)
        else:
            self.fw.op(self.fw.dve, lambda h: h.reciprocal(out, in_), reads=reads, writes=writes)

    def wload_cols(self, pool, src2d, nblk):
        t, _ = pool.next()
        ncol = src2d.shape[1] // nblk
        bufs = []
        for i in range(nblk):
            b = Buf()
            self.fw.dma(self.fw.pool, "w", t[:, :, i * ncol:(i + 1) * ncol],
                        src2d[:, i * ncol:(i + 1) * ncol].rearrange("(k p) n -> p k n", p=128), writes=[b])
            bufs.append(b)
        return t, bufs

    def wload(self, pool, src2d):
        t, b = pool.next()
        self.fw.dma(self.fw.pool, "w", t[:], src2d.rearrange("(k p) n -> p k n", p=128), writes=[b])
        return t, b

    def ph_ffn0(self, l):
        self.ffn(l, 0, 0, 1)

    def ph_ffn1(self, l):
        self.ffn(l, 1, 6, 7)

    def ffn(self, l, i, n_in, n_out):
        fw = self.fw
        T = 1024
        w1 = self.w1_d[l, i]
        w3 = self.w3_d[l, i]
        w2 = self.w2_d[l, i]
        with self.scope():
            s = self.sb("s", [128, KF, T], BF16)
            sb_ = [[Buf() for _ in range(2)] for _ in range(KF)]
            w13pool = TPool(self, "w13", [128, KD, 256], BF16, 4)
            w2pool = TPool(self, "w2", [128, KF, 256], BF16, 2)
            pre_w13 = None
            pre_w2 = None
            for half in range(2):
                t0 = half * T
                with self.scope():
                    hT = self.sb("hT", [128, KD, T], BF16)
                    hb = [[Buf() for _ in range(2)] for _ in range(KD)]
                    sapool = TPool(self, "sa", [128, 512], F32, 3)
                    for tb in range(2):
                        gtb = half * 2 + tb
                        gts = slice(gtb * 512, (gtb + 1) * 512)
                        for c in range(KD):
                            fw.op(fw.dve, lambda h: h.scalar_tensor_tensor(
                                hT[:, c, tb * 512:(tb + 1) * 512], self.x[:, c, gts], self.gcol(l, n_in, c), self.rx[:, gts],
                                ALU.mult, ALU.mult),
                                reads=[self.xb[c][gtb], self.rxb[gtb], self.vecs_b], writes=[hb[c][tb]])
                    hall = [b for c in range(KD) for b in hb[c]]
                    for mg in range(KF // 2):
                        if mg == 0 and pre_w13 is not None:
                            (w1t, w1b), (w3t, w3b) = pre_w13
                            pre_w13 = None
                        else:
                            w1t, w1b = self.wload(w13pool, w1[:, mg * 256:(mg + 1) * 256])
                            w3t, w3b = self.wload(w13pool, w3[:, mg * 256:(mg + 1) * 256])
                        if mg == KF // 2 - 2:
                            pre_w2 = self.wload(w2pool, w2[:, 0:256])
                        for mm_ in range(2):
                            m = mg * 2 + mm_
                            for tb in range(2):
                                hr = [hb[c][tb] for c in range(KD)]
                                pa, pab = self.psum()
                                fw.mm_group([(pa[:], w1t[:, k, mm_ * 128:(mm_ + 1) * 128],
                                              hT[:, k, tb * 512:(tb + 1) * 512]) for k in range(KD)],
                                            reads=[w1b] + hr, writes=[pab])
                                pbk, pbb = self.psum()
                                fw.mm_group([(pbk[:], w3t[:, k, mm_ * 128:(mm_ + 1) * 128],
                                              hT[:, k, tb * 512:(tb + 1) * 512]) for k in range(KD)],
                                            reads=[w3b] + hr, writes=[pbb])
                                sa, sab = sapool.next()
                                fw.op(fw.act, lambda h: h.activation(sa[:], pa[:], AF.Silu),
                                      reads=[pab], writes=[sab])
                                fw.op(fw.dve, lambda h: h.tensor_tensor(
                                    s[:, m, tb * 512:(tb + 1) * 512], sa[:], pbk[:], ALU.mult),
                                    reads=[sab, pbb], writes=[sb_[m][tb]])
                with self.scope():
                    y = self.sb("y", [128, KD, T], F32)
                    yb = [[Buf() for _ in range(2)] for _ in range(KD)]
                    rstd = self.sb("rstd", [128, T], F32)
                    rstd_b = Buf()
                    sqpool = TPool(self, "sq", [128, 512], BF16, 4)
                    rtpool = TPool(self, "rt", [128, 512], F32, 2)
                    tmppool = TPool(self, "tmp", [128, 512], F32, 2)
                    B06 = [0, 1, 2, 3, 4, 5]
                    stat_ps = [(self.ps[6], self.psb[6]), (self.ps[7], self.psb[7])]
                    pend = []

                    def stat_mm(tb_, mo_, sq_, sqb_):
                        fw.mm(stat_ps[tb_][0][:], self.onesD[:], sq_[:], mo_ == 0, mo_ == KD - 1,
                              reads=[sqb_, self.cb], writes=[stat_ps[tb_][1]])
                    for mo2 in range(4):
                        if mo2 == 0:
                            w2t, w2b = pre_w2
                        else:
                            w2t, w2b = self.wload(w2pool, w2[:, mo2 * 256:(mo2 + 1) * 256])
                        if mo2 == 2 and half == 0:
                            pre_w13 = (self.wload(w13pool, w1[:, 0:256]), self.wload(w13pool, w3[:, 0:256]))
                        for mm_ in range(2):
                            mo = mo2 * 2 + mm_
                            for tb in range(2):
                                p, pb = self.psum(B06)
                                fw.mm_group([(p[:], w2t[:, k, mm_ * 128:(mm_ + 1) * 128],
                                              s[:, k, tb * 512:(tb + 1) * 512]) for k in range(KF)],
                                            reads=[w2b] + [sb_[k][tb] for k in range(KF)], writes=[pb])
                                fw.op(fw.act, lambda h: h.copy(y[:, mo, tb * 512:(tb + 1) * 512], p[:]),
                                      reads=[pb], writes=[yb[mo][tb]])
                                sq, sqb = sqpool.next()
                                fw.op(fw.act, lambda h: h.activation(sq[:], p[:], AF.Square), reads=[pb], writes=[sqb])
                                pend.append((tb, mo, sq, sqb))
                                if len(pend) > 2:
                                    stat_mm(*pend.pop(0))
                    while pend:
                        stat_mm(*pend.pop(0))
                    def epi1(tb):
                        rb_ = Buf()
                        rt, rtb = rtpool.next()
                        fw.op(fw.act, lambda h: h.activation(rt[:], stat_ps[tb][0][:], AF.Ln, bias=self.epsr[:], scale=1.0),
                              reads=[stat_ps[tb][1], self.cb], writes=[rtb])
                        fw.op(fw.act, lambda h: h.activation(rstd[:, tb * 512:(tb + 1) * 512], rt[:], AF.Exp, scale=-0.5),
                              reads=[rtb], writes=[rb_])
                        return rb_

                    def epi2(tb, rb_):
                        gtb = half * 2 + tb
                        gts = slice(gtb * 512, (gtb + 1) * 512)
                        for c in range(KD):
                            tmp, tmpb = tmppool.next()
                            fw.op(fw.dve, lambda h: h.scalar_tensor_tensor(
                                tmp[:], y[:, c, tb * 512:(tb + 1) * 512], self.ghcol(l, n_out, c), rstd[:, tb * 512:(tb + 1) * 512],
                                ALU.mult, ALU.mult),
                                reads=[yb[c][tb], rb_, self.gh_b], writes=[tmpb])
                            fw.op(fw.dve, lambda h: h.tensor_tensor(
                                self.x[:, c, gts], self.x[:, c, gts], tmp[:], ALU.add),
                                reads=[tmpb, self.xb[c][gtb]], writes=[self.xb[c][gtb]])

                    r0 = epi1(0)
                    r1 = epi1(1)
                    epi2(0, r0)
                    self.update_rstd(half * 2, sqpool, rtpool)
                    epi2(1, r1)
                    self.update_rstd(half * 2 + 1, sqpool, rtpool)

    def ph_mixer(self, l):
        fw = self.fw
        win = self.win_d[l]
        lam_init = 0.8 - 0.6 * math.exp(-0.3 * l)
        with self.scope():
            R_ = self.sb("R", [128, KD, S], BF16)
            rb = [[Buf() for _ in range(4)] for _ in range(KD)]
            with self.scope():
                hT = self.sb("hTm", [128, KD, S], BF16)
                hb = [[Buf() for _ in range(4)] for _ in range(KD)]
                self.norm_full(l, 2, hT, hb)
                self.attn_core(l, win, hT, hb, R_, rb, lam_init)
                self.attn_proj_inplace(l, win, hT, hb, R_, rb)
                self.conv_branch(l, win, hT, hb, R_, rb)
                self.pool_branch(l, win, hT, hb, R_, rb)
            self.proj_norm_res(l, self.wo_d[l], R_, rb, 3)

    def attn_core(self, l, win, hT, hb, oT, ob, lam_init):
        fw = self.fw
        B6 = [4, 5, 6, 7]
        with self.scope():
            bst = self.sb("bst", [128, NH, 256], F32)
            bstb = Buf()
            b31 = self.sb("b31", [128, NH], F32)
            lamt = self.sb("lamt", [128, 256], F32)
            junk = self.sb("junk", [128, 64], F32)
            s12 = self.sb("s12", [128, 2], F32)
            e12 = self.sb("e12", [128, 2], F32)
            neglam = self.sb("neglam", [128, 1], F32)
            subS = self.sb("subS", [128, 1], F32)
            smb = Buf()
            fw.dma(fw.sp, "a", bst[:], self.bstrip_d.rearrange("p (h j) -> p h j", h=NH), writes=[bstb])
            fw.dma(fw.sp, "a", b31[:], self.b31_d, writes=[bstb])
            fw.dma(fw.sp, "a", lamt[:], self.lamb_d[:, l * 256:(l + 1) * 256], writes=[smb])
            for h_ in range(NH):
                fw.op(fw.dve, lambda h: h.tensor_scalar(bst[:, h_, :], bst[:, h_, :], b31[:, h_:h_ + 1], None, ALU.subtract),
                      reads=[bstb], writes=[bstb])
            fw.op(fw.dve, lambda h: h.scalar_tensor_tensor(junk[:], lamt[:, 0:64], 1.0, lamt[:, 64:128], ALU.mult, ALU.mult,
                                                           accum_out=s12[:, 0:1]), reads=[smb], writes=[smb])
            fw.op(fw.dve, lambda h: h.scalar_tensor_tensor(junk[:], lamt[:, 128:192], 1.0, lamt[:, 192:256], ALU.mult, ALU.mult,
                                                           accum_out=s12[:, 1:2]), reads=[smb], writes=[smb])
            fw.op(fw.act, lambda h: h.activation(e12[:], s12[:], AF.Exp), reads=[smb], writes=[smb])
            fw.op(fw.dve, lambda h: h.tensor_tensor(neglam[:], e12[:, 1:2], e12[:, 0:1], ALU.subtract), reads=[smb], writes=[smb])
            fw.op(fw.dve, lambda h: h.tensor_scalar(neglam[:], neglam[:], -lam_init, None, ALU.add), reads=[smb], writes=[smb])
            fw.op(fw.dve, lambda h: h.tensor_scalar(subS[:], self.vcol(l, V_SUB, 0), 1.0 - lam_init, None, ALU.mult),
                  reads=[smb, self.vecs_b], writes=[smb])
            wpool = TPool(self, "wqkv", [128, KD, 256], BF16, 3)
            vpool = TPool(self, "Vt", [128, 16, 256], BF16, 1)
            qpool = TPool(self, "qT", [128, S], BF16, 2)
            kpool = TPool(self, "kT", [128, S], BF16, 2)
            epool = TPool(self, "E", [128, 512], BF16, 8)
            fpool = TPool(self, "fin", [128, 512], F32, 6)
            sqpool = TPool(self, "sq", [128, 512], BF16, 2)
            acc = [self.ps[i] for i in range(4)]
            accb = [self.psb[i] for i in range(4)]
            defer = []

            def drain(n):
                for _ in range(min(n, len(defer))):
                    defer.pop(0)()
            def proj_v(hp):
                wv, wvb = self.wload(wpool, win[:, 4096 + hp * 256:4096 + (hp + 1) * 256])
                V, Vb = vpool.next()
                for tk in range(16):
                    p, pb = self.psum(B6)
                    fw.mm_group([(p[:, :256], hT[:, k, tk * 128:(tk + 1) * 128], wv[:, k, :]) for k in range(KD)],
                                reads=[wvb] + [hb[k][tk // 4] for k in range(KD)], writes=[pb])
                    fw.op(fw.dve, lambda h: h.tensor_copy(V[:, tk, :], p[:, :256]), reads=[pb], writes=[Vb])
                return V, Vb

            def qk_tasks(e, wq, wqb, wk, wkb, qT, qb, kT, kb_):
                ts_ = []
                for tb in range(4):
                    def tq(tb=tb):
                        ts = slice(tb * 512, (tb + 1) * 512)
                        p, pb = self.psum(B6)
                        fw.mm_group([(p[:], wq[:, k, e * 128:(e + 1) * 128], hT[:, k, ts]) for k in range(KD)],
                                    reads=[wqb] + [hb[k][tb] for k in range(KD)], writes=[pb])
                        fw.op(fw.dve, lambda h: h.tensor_scalar(qT[:, ts], p[:], 0.125, None, ALU.mult), reads=[pb], writes=[qb])

                    def tk_(tb=tb):
                        ts = slice(tb * 512, (tb + 1) * 512)
                        p, pb = self.psum(B6)
                        fw.mm_group([(p[:], wk[:, k, e * 128:(e + 1) * 128], hT[:, k, ts]) for k in range(KD)],
                                    reads=[wkb] + [hb[k][tb] for k in range(KD)], writes=[pb])
                        fw.op(fw.dve, lambda h: h.tensor_copy(kT[:, ts], p[:]), reads=[pb], writes=[kb_])
                    ts_ += [tq, tk_]
                return ts_

            def run_head(hd, e, V, Vb, qT, qb, kT, kb_, tasks):
                units = [(I, j) for I in range(4) for j in range(4 * I + 4)]

                def emit_scores(I, j):
                    r = j - 4 * I
                    qlo = max(r, 0) * 128
                    N = 512 - qlo
                    sps = [self.psum(B6), self.psum(B6)]
                    fw.mm_multi([(sps[m][0][:, :N], kT[m * 64:(m + 1) * 64, j * 128:(j + 1) * 128],
                                  qT[m * 64:(m + 1) * 64, I * 512 + qlo:(I + 1) * 512], True, True) for m in range(2)],
                                reads=[kb_, qb], writes=[sps[0][1], sps[1][1]])
                    es = []
                    for m in range(2):
                        sp_, spb = sps[m]
                        if r >= -1:
                            off, nn = (0, min(256, N)) if r >= 0 else (128, 128)
                            fw.op(fw.dve, lambda h: h.tensor_tensor(sp_[:, :nn], sp_[:, :nn], bst[:, hd, off:off + nn], ALU.add),
                                  reads=[spb, bstb], writes=[spb])
                        e_, eb = epool.next()
                        fw.op(fw.act, lambda h: h.activation(e_[:, :N], sp_[:, :N], AF.Exp), reads=[spb], writes=[eb])
                        es.append((e_, eb))
                    return es, qlo, N

                pending = [emit_scores(*units[0]), emit_scores(*units[1])]
                for ui, (I, j) in enumerate(units):
                    es, qlo, N = pending.pop(0)
                    if ui + 2 < len(units):
                        pending.append(emit_scores(*units[ui + 2]))
                    nkb = 4 * I + 4
                    mms = []
                    for m in range(2):
                        mms.append((acc[m][:, qlo:512], V[:, j, e * 128:(e + 1) * 128], es[m][0][:, :N], j == 0, j == nkb - 1))
                        mms.append((acc[2 + m][:, qlo:512], self.ones1[:], es[m][0][:, :N], j == 0, j == nkb - 1))
                    fw.mm_multi(mms, reads=[Vb, self.cb, es[0][1], es[1][1]], writes=accb)
                    drain(2)
                    if tasks and ui % 4 == 3:
                        tasks.pop(0)()
                    if j != nkb - 1:
                        continue
                    drain(len(defer))
                    self.attn_finalize(defer, acc, accb, fpool, sqpool, B6, neglam, subS, smb, oT, ob, hd, I)

            V, Vb = proj_v(0)
            wq, wqb = self.wload(wpool, win[:, 2048:2048 + 256])
            wk, wkb = self.wload(wpool, win[:, 3072:3072 + 256])
            qT, qb = qpool.next()
            kT, kb_ = kpool.next()
            for t_ in qk_tasks(0, wq, wqb, wk, wkb, qT, qb, kT, kb_):
                t_()
            for hd in range(NH):
                hp, e = divmod(hd, 2)
                tasks = []
                nxt_qk = None
                if hd + 1 < NH:
                    nhp, ne = divmod(hd + 1, 2)
                    if ne == 0:
                        wq, wqb = self.wload(wpool, win[:, 2048 + nhp * 256:2048 + (nhp + 1) * 256])
                        wk, wkb = self.wload(wpool, win[:, 3072 + nhp * 256:3072 + (nhp + 1) * 256])
                    nq, nqb = qpool.next()
                    nk, nkb_ = kpool.next()
                    tasks = qk_tasks(ne, wq, wqb, wk, wkb, nq, nqb, nk, nkb_)
                    nxt_qk = (nq, nqb, nk, nkb_)
                run_head(hd, e, V, Vb, qT, qb, kT, kb_, tasks)
                while tasks:
                    tasks.pop(0)()
                if e == 1 and hd + 1 < NH:
                    V, Vb = proj_v(hp + 1)
                if nxt_qk is not None:
                    qT, qb, kT, kb_ = nxt_qk
            drain(len(defer))

    def attn_finalize(self, defer, acc, accb, fpool, sqpool, B6, neglam, subS, smb, oT, ob, hd, I):
        fw = self.fw
        z0, z0b = fpool.next()
        fw.op(fw.act, lambda h: h.copy(z0[:], acc[2][:]), reads=[accb[2]], writes=[z0b])
        z1, z1b = fpool.next()
        fw.op(fw.act, lambda h: h.copy(z1[:], acc[3][:]), reads=[accb[3]], writes=[z1b])
        t0, t0b = fpool.next()
        fw.op(fw.dve, lambda h: h.tensor_copy(t0[:], acc[0][:]), reads=[accb[0]], writes=[t0b])
        t1, t1b = fpool.next()
        fw.op(fw.dve, lambda h: h.tensor_copy(t1[:], acc[1][:]), reads=[accb[1]], writes=[t1b])
        st = {}

        def s1():
            fw.op(fw.dve, lambda h: h.tensor_tensor(t0[:], t0[:], z1[:], ALU.mult), reads=[t0b, z1b], writes=[t0b])

        def s2():
            fw.op(fw.dve, lambda h: h.tensor_tensor(t1[:], t1[:], z0[:], ALU.mult), reads=[t1b, z0b], writes=[t1b])

        def s3():
            fw.op(fw.dve, lambda h: h.scalar_tensor_tensor(t0[:], t1[:], neglam[:, 0:1], t0[:], ALU.mult, ALU.add),
                  reads=[t0b, t1b, smb], writes=[t0b])

        def s4():
            sq, sqb = sqpool.next()
            fw.op(fw.act, lambda h: h.activation(sq[:], t0[:], AF.Square), reads=[t0b], writes=[sqb])
            st["sq"] = (sq, sqb)

        def s5():
            sq, sqb = st["sq"]
            pst, pstb = self.psum(B6)
            fw.mm(pst[:], self.ones128[:], sq[:], True, True, reads=[sqb, self.cb], writes=[pstb])
            st["pst"] = (pst, pstb)

        def s6():
            fw.op(fw.dve, lambda h: h.tensor_tensor(z0[:], z0[:], z1[:], ALU.mult), reads=[z0b, z1b], writes=[z0b])

        def s7():
            fw.op(fw.dve, lambda h: h.scalar_tensor_tensor(z0[:], z0[:], RMS_EPS, z0[:], ALU.mult, ALU.mult),
                  reads=[z0b], writes=[z0b])

        def s8():
            pst, pstb = st["pst"]
            fw.op(fw.dve, lambda h: h.tensor_tensor(z0[:], pst[:], z0[:], ALU.add), reads=[pstb, z0b], writes=[z0b])

        def s9():
            fw.op(fw.act, lambda h: h.activation(z0[:], z0[:], AF.Ln), reads=[z0b], writes=[z0b])

        def s10():
            fw.op(fw.act, lambda h: h.activation(z0[:], z0[:], AF.Exp, scale=-0.5), reads=[z0b], writes=[z0b])

        def s11():
            fw.op(fw.dve, lambda h: h.scalar_tensor_tensor(oT[:, hd, I * 512:(I + 1) * 512], t0[:], subS[:, 0:1], z0[:],
                                                           ALU.mult, ALU.mult),
                  reads=[t0b, z0b, smb], writes=[ob[hd][I]])

        defer.extend([s1, s2, s3, s4, s6, s7, s5, s8, s9, s10, s11])

    def attn_proj_inplace(self, l, win, hT, hb, R_, rb):
        fw = self.fw
        goff = 6144 + D
        with self.scope():
            wpool = TPool(self, "wres2", [128, KD, D], BF16, 2)
            mt = self.sb("mtmp", [128, KD, 512], BF16)
            mtb = Buf()
            sgpool = TPool(self, "sg", [128, 512], F32, 3)
            wa, _ = wpool.next()
            wg, _ = wpool.next()
            wabs, wgbs = [], []
            for i in range(4):
                for (t_, s_, bl) in ((wa, self.ao_d[l], wabs), (wg, win[:, goff:goff + D], wgbs)):
                    b = Buf()
                    fw.dma(fw.pool, "w", t_[:, :, i * 256:(i + 1) * 256],
                           s_[:, i * 256:(i + 1) * 256].rearrange("(k p) n -> p k n", p=128), writes=[b])
                    bl.append(b)
            for tb in range(4):
                ts = slice(tb * 512, (tb + 1) * 512)
                for mo in range(KD):
                    pa, pab = self.psum()
                    fw.mm_group([(pa[:], wa[:, k, mo * 128:(mo + 1) * 128], R_[:, k, ts]) for k in range(KD)],
                                reads=[wabs[mo // 2]] + [rb[k][tb] for k in range(KD)], writes=[pab])
                    pg, pgb = self.psum()
                    fw.mm_group([(pg[:], wg[:, k, mo * 128:(mo + 1) * 128], hT[:, k, ts]) for k in range(KD)],
                                reads=[wgbs[mo // 2]] + [hb[k][tb] for k in range(KD)], writes=[pgb])
                    sg, sgb = sgpool.next()
                    fw.op(fw.act, lambda h: h.activation(sg[:], pg[:], AF.Sigmoid), reads=[pgb], writes=[sgb])
                    fw.op(fw.dve, lambda h: h.tensor_tensor(mt[:, mo, :], pa[:], sg[:], ALU.mult), reads=[pab, sgb], writes=[mtb])
                fw.op(fw.dve, lambda h: h.tensor_copy(R_[:, :, ts], mt[:]), reads=[mtb], writes=[rb[k][tb] for k in range(KD)])

    def gate_accum(self, l, w2d, inT, inb, goff, win, hT, hb, R_, rb, wpool, sgpool):
        fw = self.fw
        for mo2 in range(4):
            wa, wab = self.wload(wpool, w2d[:, mo2 * 256:(mo2 + 1) * 256])
            wg, wgb = self.wload(wpool, win[:, goff + mo2 * 256:goff + (mo2 + 1) * 256])
            for mm_ in range(2):
                mo = mo2 * 2 + mm_
                for q in range(4):
                    qs = slice(q * 512, (q + 1) * 512)
                    pa, pab = self.psum()
                    fw.mm_group([(pa[:], wa[:, k, mm_ * 128:(mm_ + 1) * 128], inT[:, k, qs]) for k in range(KD)],
                                reads=[wab] + [inb(k, q) for k in range(KD)], writes=[pab])
                    pg, pgb = self.psum()
                    fw.mm_group([(pg[:], wg[:, k, mm_ * 128:(mm_ + 1) * 128], hT[:, k, qs]) for k in range(KD)],
                                reads=[wgb] + [hb[k][q] for k in range(KD)], writes=[pgb])
                    sg, sgb = sgpool.next()
                    fw.op(fw.act, lambda h: h.activation(sg[:], pg[:], AF.Sigmoid), reads=[pgb], writes=[sgb])
                    fw.op(fw.dve, lambda h: h.tensor_tensor(sg[:], pa[:], sg[:], ALU.mult), reads=[pab, sgb], writes=[sgb])
                    fw.op(fw.dve, lambda h: h.tensor_tensor(R_[:, mo, qs], R_[:, mo, qs], sg[:], ALU.add),
                          reads=[sgb, rb[mo][q]], writes=[rb[mo][q]])

    def conv_branch(self, l, win, hT, hb, R_, rb):
        fw = self.fw
        with self.scope():
            zc = self.sb("zc", [128, KD, S], BF16)
            zcb = [[Buf() for _ in range(4)] for _ in range(KD)]
            wpool = TPool(self, "wcv", [128, KD, 256], BF16, 3)
            dgpool = TPool(self, "diag", [128, CK, 128], BF16, 1)
            zpool = TPool(self, "z", [128, 30 + 512], BF16, 3)
            zhalo = self.sb("zhalo", [128, 2, 30], BF16)
            zhb = [Buf() for _ in range(2)]
            sgpool = TPool(self, "sg", [128, 512], F32, 2)
            sqpool = TPool(self, "sq", [128, 512], BF16, 3)
            stpool = TPool(self, "st", [128, 512], F32, 3)
            for cp in range(4):
                wa, wab = self.wload(wpool, win[:, cp * 256:(cp + 1) * 256])
                wg, wgb = self.wload(wpool, win[:, D + cp * 256:D + (cp + 1) * 256])
                for cc in range(2):
                    c = cp * 2 + cc
                    dg, dgb = dgpool.next()
                    o = V_CDW + c * CK
                    fw.op(fw.dve, lambda h: h.tensor_tensor(
                        dg[:], self.identb[:].unsqueeze(1).to_broadcast([128, CK, 128]),
                        self.vecs[:, o:o + CK].unsqueeze(2).to_broadcast([128, CK, 128]), ALU.mult),
                        reads=[self.cb, self.vecs_b], writes=[dgb])
                    def produce(q):
                        qs = slice(q * 512, (q + 1) * 512)
                        hr = [hb[k][q] for k in range(KD)]
                        z, zb = zpool.next()
                        if q == 0:
                            fw.op(fw.dve, lambda h: h.memset(z[:, 0:30], 0.0), writes=[zb])
                        else:
                            fw.op(fw.dve, lambda h: h.tensor_copy(z[:, 0:30], zhalo[:, cc, :]), reads=[zhb[cc]], writes=[zb])
                        pa, pab = self.psum()
                        fw.mm_group([(pa[:], wa[:, k, cc * 128:(cc + 1) * 128], hT[:, k, qs]) for k in range(KD)],
                                    reads=[wab] + hr, writes=[pab])
                        pg, pgb = self.psum()
                        fw.mm_group([(pg[:], wg[:, k, cc * 128:(cc + 1) * 128], hT[:, k, qs]) for k in range(KD)],
                                    reads=[wgb] + hr, writes=[pgb])
                        sg, sgb = sgpool.next()
                        fw.op(fw.act, lambda h: h.activation(sg[:], pg[:], AF.Sigmoid), reads=[pgb], writes=[sgb])
                        fw.op(fw.dve, lambda h: h.tensor_tensor(z[:, 30:542], pa[:], sg[:], ALU.mult), reads=[pab, sgb], writes=[zb])
                        if q < 3:
                            fw.op(fw.dve, lambda h: h.tensor_copy(zhalo[:, cc, :], z[:, 512:542]), reads=[zb], writes=[zhb[cc]])
                        return z, zb

                    def consume(q, z, zb):
                        qs = slice(q * 512, (q + 1) * 512)
                        pc, pcb = self.psum()
                        fw.mm_group([(pc[:], dg[:, k, :], z[:, k:k + 512]) for k in range(CK)], reads=[dgb, zb], writes=[pcb])
                        fw.op(fw.act, lambda h: h.activation(zc[:, c, qs], pc[:], AF.Identity, bias=self.vcol(l, V_CDB, c), scale=1.0),
                              reads=[pcb, self.vecs_b], writes=[zcb[c][q]])

                    nz = produce(0)
                    for q in range(4):
                        cz = nz
                        if q < 3:
                            nz = produce(q + 1)
                        consume(q, *cz)
            for q in range(4):
                qs = slice(q * 512, (q + 1) * 512)
                pm, pmb = self.psum()
                pv, pvb = self.psum()
                for c in range(KD):
                    fw.mm(pm[:], self.onesD[:], zc[:, c, qs], c == 0, c == KD - 1, reads=[zcb[c][q], self.cb], writes=[pmb])
                    zsq, zsqb = sqpool.next()
                    fw.op(fw.act, lambda h: h.activation(zsq[:], zc[:, c, qs], AF.Square), reads=[zcb[c][q]], writes=[zsqb])
                    fw.mm(pv[:], self.onesD[:], zsq[:], c == 0, c == KD - 1, reads=[zsqb, self.cb], writes=[pvb])
                mean, meanb = stpool.next()
                fw.op(fw.act, lambda h: h.copy(mean[:], pm[:]), reads=[pmb], writes=[meanb])
                var, varb = stpool.next()
                fw.op(fw.dve, lambda h: h.tensor_tensor(var[:], mean[:], mean[:], ALU.mult), reads=[meanb], writes=[varb])
                fw.op(fw.dve, lambda h: h.tensor_tensor(var[:], pv[:], var[:], ALU.subtract), reads=[pvb, varb], writes=[varb])
                fw.op(fw.act, lambda h: h.activation(var[:], var[:], AF.Sqrt, bias=self.epsl[:], scale=1.0), reads=[varb, self.cb], writes=[varb])
                self.recip(var[:], var[:], [varb], [varb])
                for c in range(KD):
                    t, tb_ = sgpool.next()
                    fw.op(fw.dve, lambda h: h.tensor_tensor(t[:], zc[:, c, qs], mean[:], ALU.subtract), reads=[zcb[c][q], meanb], writes=[tb_])
                    fw.op(fw.dve, lambda h: h.scalar_tensor_tensor(t[:], t[:], self.vcol(l, V_LNG, c), var[:], ALU.mult, ALU.mult),
                          reads=[tb_, varb, self.vecs_b], writes=[tb_])
                    fw.op(fw.act, lambda h: h.activation(zc[:, c, qs], t[:], AF.Silu, bias=self.vcol(l, V_LNB, c), scale=1.0),
                          reads=[tb_, self.vecs_b], writes=[zcb[c][q]])
            self.gate_accum(l, self.cpw_d[l], zc, lambda k, q: zcb[k][q], 6144, win, hT, hb, R_, rb, wpool, sgpool)

    def pool_branch(self, l, win, hT, hb, R_, rb):
        fw = self.fw
        HL = 15
        with self.scope():
            y1 = self.sb("y1", [128, KD, S], BF16)
            y1b = [[Buf() for _ in range(4)] for _ in range(KD)]
            wpool = TPool(self, "wpl", [128, KD, 256], BF16, 3)
            pwpool = TPool(self, "pwt", [128, 2, 256], BF16, 2)
            upool = TPool(self, "u", [128, HL + 512], F32, 3)
            spool = TPool(self, "ws", [128, HL + 512], F32, 3)
            for t_, b_ in zip(spool.tiles, spool.bufs):
                fw.op(fw.dve, lambda h: h.memset(t_[:, 0:HL], 0.0), writes=[b_])
            uhalo = self.sb("uhalo", [128, 2, HL], F32)
            uhb = [Buf() for _ in range(2)]
            plpool = TPool(self, "pooled", [128, 2, 512], BF16, 2)
            sgpool = TPool(self, "sg", [128, 512], F32, 2)
            fx = self.sb("fx", [128, 16], F32)
            fxb = Buf()
            pcb_ = Buf()
            pc = self.sb("poolc", [128, 64], F32)
            fw.dma(fw.sp, "a", pc[:], self.poolc_d, writes=[pcb_])
            for g in range(4):
                w = (2, 4, 8, 16)[g]
                wu, wub = self.wload(wpool, win[:, 5120 + g * 256:5120 + (g + 1) * 256])
                pwt, pwtb = pwpool.next()
                fw.dma(fw.pool, "w", pwt[:], self.pw_d[l, g].rearrange("(k p) n -> p k n", p=128), writes=[pwtb])
                for q in range(4):
                    qs = slice(q * 512, (q + 1) * 512)
                    hr = [hb[k][q] for k in range(KD)]
                    pl, plb = plpool.next()
                    us = []
                    for cc in range(2):
                        u, ub = upool.next()
                        if q == 0:
                            fw.op(fw.dve, lambda h: h.memset(u[:, 0:HL], 0.0), writes=[ub])
                        else:
                            fw.op(fw.dve, lambda h: h.tensor_copy(u[:, 0:HL], uhalo[:, cc, :]), reads=[uhb[cc]], writes=[ub])
                        pu, pub = self.psum()
                        fw.mm_group([(pu[:], wu[:, k, cc * 128:(cc + 1) * 128], hT[:, k, qs]) for k in range(KD)],
                                    reads=[wub] + hr, writes=[pub])
                        fw.op(fw.act, lambda h: h.copy(u[:, HL:HL + 512], pu[:]), reads=[pub], writes=[ub])
                        if q < 3:
                            fw.op(fw.dve, lambda h: h.tensor_copy(uhalo[:, cc, :], u[:, 512:512 + HL]), reads=[ub], writes=[uhb[cc]])
                        us.append((u, ub))
                    for cc in range(2):
                        u, ub = us[cc]
                        cur, curb = u, ub
                        d = 1
                        while d < w:
                            nx, nxb = spool.next()
                            fw.op(fw.dve, lambda h: h.tensor_tensor(nx[:, d:HL + 512], cur[:, d:HL + 512], cur[:, 0:HL + 512 - d], ALU.add),
                                  reads=[curb], writes=[nxb])
                            cur, curb = nx, nxb
                            d *= 2
                        fw.op(fw.dve, lambda h: h.scalar_tensor_tensor(pl[:, cc, :], cur[:, HL:HL + 512], 1.0 / w, u[:, HL:HL + 512],
                                                                       ALU.mult, ALU.subtract),
                              reads=[curb, ub], writes=[plb])
                        if q == 0:
                            fw.op(fw.dve, lambda h: h.tensor_tensor(fx[:], cur[:, HL:HL + 16], pc[:, g * 16:(g + 1) * 16], ALU.mult),
                                  reads=[curb, pcb_], writes=[fxb])
                            fw.op(fw.dve, lambda h: h.tensor_tensor(pl[:, cc, 0:16], fx[:], u[:, HL:HL + 16], ALU.subtract),
                                  reads=[fxb, ub, plb], writes=[plb])
                    for oc in range(2):
                        c = g * 2 + oc
                        py, pyb = self.psum()
                        fw.mm_group([(py[:], pwt[:, kc, oc * 128:(oc + 1) * 128], pl[:, kc, :]) for kc in range(2)],
                                    reads=[pwtb, plb], writes=[pyb])
                        fw.op(fw.act, lambda h: h.mul(y1[:, c, qs], py[:], self.vcol(l, V_PSC, c)), reads=[pyb, self.vecs_b], writes=[y1b[c][q]])
            self.gate_accum(l, self.po_d[l], y1, lambda k, q: y1b[k][q], 6144 + 2 * D, win, hT, hb, R_, rb, wpool, sgpool)

    def norm_full(self, l, n_norm, hT, hb):
        fw = self.fw
        for tb in range(4):
            ts = slice(tb * 512, (tb + 1) * 512)
            for c in range(KD):
                fw.op(fw.dve, lambda h: h.scalar_tensor_tensor(
                    hT[:, c, ts], self.x[:, c, ts], self.gcol(l, n_norm, c), self.rx[:, ts], ALU.mult, ALU.mult),
                    reads=[self.xb[c][tb], self.rxb[tb], self.vecs_b], writes=[hb[c][tb]])

    def proj_norm_res(self, l, w2d, inT, inb, n_norm, pre=None):
        fw = self.fw
        with self.scope():
            wpool = TPool(self, "wres", [128, KD, D], BF16, 1) if pre is None else None
            ypool = TPool(self, "yt", [128, KD, 512], F32, 3)
            rpool = TPool(self, "rstdt", [128, 512], F32, 2)
            sqpool = TPool(self, "sq", [128, 512], BF16, 5)
            rtpool = TPool(self, "rt", [128, 512], F32, 2)
            tmppool = TPool(self, "tmp", [128, 512], F32, 3)
            wt, wbs = self.wload_cols(wpool, w2d, 4) if pre is None else pre

            B05 = [0, 1, 2, 3, 4]

            def mm_stage(tb):
                ts = slice(tb * 512, (tb + 1) * 512)
                yt, ytb = ypool.next()
                sp_, spb_ = self.ps[5 + tb % 3], self.psb[5 + tb % 3]
                pend = []

                def stat_mm(mo_, sq_, sqb_):
                    fw.mm(sp_[:], self.onesD[:], sq_[:], mo_ == 0, mo_ == KD - 1, reads=[sqb_, self.cb], writes=[spb_])
                for mo in range(KD):
                    p, pb = self.psum(B05)
                    fw.mm_group([(p[:], wt[:, k, mo * 128:(mo + 1) * 128], inT[:, k, ts]) for k in range(KD)],
                                reads=[wbs[mo // 2]] + [inb[k][tb] for k in range(KD)], writes=[pb])
                    fw.op(fw.act, lambda h: h.copy(yt[:, mo, :], p[:]), reads=[pb], writes=[ytb])
                    sq, sqb = sqpool.next()
                    fw.op(fw.act, lambda h: h.activation(sq[:], p[:], AF.Square), reads=[pb], writes=[sqb])
                    pend.append((mo, sq, sqb))
                    if len(pend) > 2:
                        stat_mm(*pend.pop(0))
                while pend:
                    stat_mm(*pend.pop(0))
                return yt, ytb

            def epi1(tb, yt, ytb):
                rstd, rstd_b = rpool.next()
                sp_, spb_ = self.ps[5 + tb % 3], self.psb[5 + tb % 3]
                rt, rtb = rtpool.next()
                fw.op(fw.act, lambda h: h.activation(rt[:], sp_[:], AF.Ln, bias=self.epsr[:], scale=1.0),
                      reads=[spb_, self.cb], writes=[rtb])
                fw.op(fw.act, lambda h: h.activation(rstd[:], rt[:], AF.Exp, scale=-0.5), reads=[rtb], writes=[rstd_b])
                return rstd, rstd_b

            def epi2(tb, yt, ytb, rstd, rstd_b):
                ts = slice(tb * 512, (tb + 1) * 512)
                for c in range(KD):
                    tmp, tmpb = tmppool.next()
                    fw.op(fw.dve, lambda h: h.scalar_tensor_tensor(
                        tmp[:], yt[:, c, :], self.gcol(l, n_norm, c), rstd[:], ALU.mult, ALU.mult),
                        reads=[ytb, rstd_b, self.vecs_b], writes=[tmpb])
                    fw.op(fw.dve, lambda h: h.tensor_tensor(
                        self.x[:, c, ts], self.x[:, c, ts], tmp[:], ALU.add),
                        reads=[tmpb, self.xb[c][tb]], writes=[self.xb[c][tb]])

            ys = [mm_stage(0), mm_stage(1)]
            r0 = epi1(0, *ys[0])
            rs_ = {0: r0}
            for tb in range(4):
                if tb + 2 < 4:
                    ys.append(mm_stage(tb + 2))
                epi2(tb, *ys[tb], *rs_[tb])
                if tb + 1 < 4:
                    rs_[tb + 1] = epi1(tb + 1, *ys[tb + 1])
                self.update_rstd(tb, sqpool, rtpool, banks=B05)

    def ph_xattn(self, l):
        fw = self.fw
        xq = self.xq_d[l]
        xkv = self.xkv_d[l]
        with self.scope():
          oT = self.sb("oTx", [128, KD, S], BF16)
          ob = [[Buf() for _ in range(4)] for _ in range(KD)]
          xo_pool = TPool(self, "wresx", [128, KD, D], BF16, 1)
          with self.scope():
            hT = self.sb("hTx", [128, KD, S], BF16)
            hb = [[Buf() for _ in range(4)] for _ in range(KD)]
            KxT = self.sb("KxT", [128, KD, MEM], BF16)
            kxb = Buf()
            Vx = self.sb("Vx", [128, 2, D], BF16)
            vxb = Buf()
            with self.scope():
                memf = self.sb("memf", [128, KD, MEM], F32)
                memfb = Buf()
                memn = self.sb("memn", [128, KD, MEM], BF16)
                memnb = Buf()
                rstd = self.sb("rstdm", [128, MEM], F32)
                rstd_b = Buf()
                sqpool = TPool(self, "sq", [128, 512], BF16, 3)
                rtpool = TPool(self, "rt", [128, 512], F32, 2)
                wpool = TPool(self, "wkv", [128, KD, 512], BF16, 2)
                fw.dma(fw.sp, "a", memf[:], self.memT_d.rearrange("(c p) t -> p c t", p=128), writes=[memfb])
                srcs = [(lambda t_, c=c: memf[:, c, :], lambda t_: [memfb]) for c in range(KD)]
                self.rstd_blocks(srcs, 1, rstd, rstd_b, self.onesD[:], self.epsr[:], sqpool, rtpool, W=MEM)
                for c in range(KD):
                    fw.op(fw.dve, lambda h: h.scalar_tensor_tensor(
                        memn[:, c, :], memf[:, c, :], self.vcol(l, V_MN, c), rstd[:], ALU.mult, ALU.mult),
                        reads=[memfb, rstd_b, self.vecs_b], writes=[memnb])
                for cg in range(2):
                    wt, wb = self.wload(wpool, xkv[:, cg * 512:(cg + 1) * 512])
                    for mm_ in range(4):
                        mc = cg * 4 + mm_
                        p, pb = self.psum()
                        fw.mm_group([(p[:, :MEM], wt[:, k, mm_ * 128:(mm_ + 1) * 128], memn[:, k, :]) for k in range(KD)],
                                    reads=[wb, memnb], writes=[pb])
                        fw.op(fw.act, lambda h: h.copy(KxT[:, mc, :], p[:, :MEM]), reads=[pb], writes=[kxb])
                for cg in range(2):
                    wt, wb = self.wload(wpool, xkv[:, D + cg * 512:D + (cg + 1) * 512])
                    for mb in range(2):
                        p, pb = self.psum()
                        fw.mm_group([(p[:], memn[:, k, mb * 128:(mb + 1) * 128], wt[:, k, :]) for k in range(KD)],
                                    reads=[wb, memnb], writes=[pb])
                        fw.op(fw.act, lambda h: h.copy(Vx[:, mb, cg * 512:(cg + 1) * 512], p[:]), reads=[pb], writes=[vxb])
            self.norm_full(l, 4, hT, hb)
            with self.scope():
                wqpool = TPool(self, "wq", [128, KD, 256], BF16, 2)
                qpool = TPool(self, "qTh", [128, 2, S], BF16, 2)
                epool = TPool(self, "E", [128, 512], BF16, 6)
                rzpool = TPool(self, "rz", [128, 512], F32, 2)
                scale = 1.0 / 16.0
                def q_tasks(hd_, wq, wqb, qT, qb):
                    ts_ = []
                    for cc in range(2):
                        for tb in range(4):
                            def t_(cc=cc, tb=tb):
                                ts = slice(tb * 512, (tb + 1) * 512)
                                p, pb = self.psum()
                                fw.mm_group([(p[:], wq[:, k, cc * 128:(cc + 1) * 128], hT[:, k, ts]) for k in range(KD)],
                                            reads=[wqb] + [hb[k][tb] for k in range(KD)], writes=[pb])
                                fw.op(fw.act, lambda h: h.copy(qT[:, cc, ts], p[:]), reads=[pb], writes=[qb])
                            ts_.append(t_)
                    return ts_

                wq, wqb = self.wload(wqpool, xq[:, 0:256])
                qT, qb = qpool.next()
                for t_ in q_tasks(0, wq, wqb, qT, qb):
                    t_()
                xo_pre = self.wload_cols(xo_pool, self.xo_d[l], 4)
                for hd in range(XH):
                    tasks = []
                    nq = None
                    if hd + 1 < XH:
                        wqn, wqnb = self.wload(wqpool, xq[:, (hd + 1) * 256:(hd + 2) * 256])
                        nq = qpool.next()
                        tasks = q_tasks(hd + 1, wqn, wqnb, nq[0], nq[1])

                    def emit_s(tb):
                        ts = slice(tb * 512, (tb + 1) * 512)
                        es = []
                        for mb in range(2):
                            p, pb = self.psum()
                            fw.mm_group([(p[:], KxT[:, hd * 2 + cc, mb * 128:(mb + 1) * 128], qT[:, cc, ts]) for cc in range(2)],
                                        reads=[kxb, qb], writes=[pb])
                            e, eb = epool.next()
                            fw.op(fw.act, lambda h: h.activation(e[:], p[:], AF.Exp, scale=scale), reads=[pb], writes=[eb])
                            es.append((e, eb))
                        return es

                    nxt = emit_s(0)
                    for tb in range(4):
                        ts = slice(tb * 512, (tb + 1) * 512)
                        es = nxt
                        if tb < 3:
                            nxt = emit_s(tb + 1)
                        for _ in range(2):
                            if tasks:
                                tasks.pop(0)()
                        pz, pzb = self.psum()
                        fw.mm_group([(pz[:], self.ones1[:], es[mb][0][:]) for mb in range(2)],
                                    reads=[es[0][1], es[1][1], self.cb], writes=[pzb])
                        rz, rzb = rzpool.next()
                        self.recip(rz[:], pz[:], [pzb], [rzb])
                        for cc in range(2):
                            po, pob = self.psum()
                            fw.mm_group([(po[:], Vx[:, mb, hd * 256 + cc * 128:hd * 256 + (cc + 1) * 128], es[mb][0][:])
                                         for mb in range(2)], reads=[vxb, es[0][1], es[1][1]], writes=[pob])
                            fw.op(fw.dve, lambda h: h.tensor_tensor(oT[:, hd * 2 + cc, ts], po[:], rz[:], ALU.mult),
                                  reads=[pob, rzb], writes=[ob[hd * 2 + cc][tb]])
                    while tasks:
                        tasks.pop(0)()
                    if nq is not None:
                        qT, qb = nq
          self.proj_norm_res(l, self.xo_d[l], oT, ob, 5, pre=xo_pre)


def _t5_bucket_np(n):
    n = np.maximum(n, 0)
    max_exact = 16
    nf = np.maximum(n, 1).astype(np.float32)
    large = max_exact + (np.log(nf / max_exact) / math.log(128 / max_exact) * (32 - max_exact)).astype(np.int32)
    large = np.minimum(large, 31)
    return np.where(n < max_exact, n, large)


def _prep_shared(inp):
    f = np.float32
    sh = {}
    vecs = np.zeros((128, L * V_PER), f)
    for l in range(L):
        o = l * V_PER
        vecs[:, o + V_NG:o + V_NG + 64] = inp["norm_g"][l].reshape(8, 8, 128).transpose(2, 0, 1).reshape(128, 64)
        vecs[:, o + V_CDW:o + V_CDW + 8 * CK] = inp["conv_dw"][l].reshape(CK, 8, 128).transpose(2, 1, 0).reshape(128, 8 * CK)
        for off, key in ((V_CDB, "conv_dw_b"), (V_LNG, "conv_ln_g"), (V_LNB, "conv_ln_b"),
                         (V_PSC, "pool_scale"), (V_MN, "mem_norm")):
            vecs[:, o + off:o + off + 8] = inp[key][l].reshape(8, 128).T
        vecs[:, o + V_SUB] = inp["attn_subln"][l]
    sh["vecs"] = vecs
    sh["lamb"] = np.ascontiguousarray(np.broadcast_to(inp["attn_lam"].reshape(1, L * 256), (128, L * 256))).astype(f)
    k = np.arange(128)[:, None]
    j = np.arange(256)[None, :]
    dist = j - k
    idx = _t5_bucket_np(dist)
    g = inp["rel_bias"][idx]
    g = np.where((dist >= 0)[:, :, None], g, f(NEG))
    sh["bstrip"] = np.ascontiguousarray(g.transpose(0, 2, 1).reshape(128, NH * 256)).astype(f)
    sh["b31b"] = np.ascontiguousarray(np.broadcast_to(inp["rel_bias"][31][None, :], (128, NH))).astype(f)
    sh["ident"] = np.eye(128, dtype=f)
    pc = np.zeros((128, 64), f)
    for gi, w in enumerate((2, 4, 8, 16)):
        pc[:, gi * 16:(gi + 1) * 16] = 1.0 / np.minimum(np.arange(16) + 1, w)
    sh["poolc"] = pc
    for key in ("ffn_w1", "ffn_w3", "ffn_w2", "w_in", "conv_pw", "attn_o", "pool_w", "pool_o", "w_out",
                "xattn_q", "xattn_kv", "xattn_o"):
        sh[key] = np.ascontiguousarray(inp[key], dtype=f)
    return sh


_CACHE = {}


def _get_nc(depth, plan=None):
    key = (depth, tuple(plan) if plan else None)
    if key not in _CACHE:
        nc = bass.Bass("TRN2", target_bir_lowering=False)
        kb = KB(nc, depth, plan)
        kb.build()
        _CACHE[key] = (nc, kb)
    return _CACHE[key]


def run(inputs, depth=L, plan=None, cores=8, trace=False):
    inp = {k: np.asarray(v) for k, v in inputs.items()}
    nc, kb = _get_nc(depth, plan)
    sh = _prep_shared(inp)
    in_maps = []
    for b in range(cores):
        m = dict(sh)
        m["xT"] = np.ascontiguousarray(inp["x"][b].T, dtype=np.float32)
        m["memT"] = np.ascontiguousarray(inp["mem"][b].T, dtype=np.float32)
        in_maps.append(m)
    res = run_bass_kernel_spmd(nc, in_maps, core_ids=list(range(cores)), trace=trace)
    out = np.stack([np.ascontiguousarray(r["outT"].T) for r in res.results], axis=0)
    return out, res, kb


def kernel(**inputs):
    out, _, _ = run(inputs)
    return out.astype(np.float32)
```

```python
import math
from contextlib import ExitStack, contextmanager

import numpy as np
import concourse.bass as bass
import concourse.mybir as mybir
from concourse.bass_utils import run_bass_kernel_spmd

F32 = mybir.dt.float32
BF16 = mybir.dt.bfloat16
AF = mybir.ActivationFunctionType
ALU = mybir.AluOpType

D = 1024
S = 2048
L = 4
DFF = 2816
KD = 8
KF = 22
MEM = 256
NH = 8
XH = 4
CK = 31
IN_COLS = 9216
RMS_EPS = 1e-6
LN_EPS = 1e-5
NEG = -30000.0
FAST_RECIP = False

V_NG = 0
V_CDW = 64
V_CDB = V_CDW + 8 * CK
V_LNG = V_CDB + 8
V_LNB = V_LNG + 8
V_PSC = V_LNB + 8
V_MN = V_PSC + 8
V_SUB = V_MN + 8
V_PER = V_SUB + 1


class Stream:
    def __init__(self, name, sem, step):
        self.name = name
        self.sem = sem
        self.step = step
        self.count = 0


class Engine(Stream):
    def __init__(self, name, sem, handle):
        super().__init__(name, sem, 1)
        self.h = handle
        self.seen = {}
        self.ninst = 0


class Buf:
    __slots__ = ("w", "r")

    def __init__(self):
        self.w = None
        self.r = {}


class FW:
    def __init__(self, nc, ctx):
        self.nc = nc

        def sem(n):
            return ctx.enter_context(nc.semaphore(n))

        self.pe = Engine("pe", sem("s_pe"), nc.tensor)
        self.act = Engine("act", sem("s_act"), nc.scalar)
        self.dve = Engine("dve", sem("s_dve"), nc.vector)
        self.pool = Engine("pool", sem("s_pool"), nc.gpsimd)
        self.sp = Engine("sp", sem("s_sp"), nc.sync)
        self.engines = [self.pe, self.act, self.dve, self.pool, self.sp]
        self.dq = {}
        self.dqi = {}
        for n, k in (("w", 8), ("a", 8), ("o", 1)):
            self.dq[n] = [Stream(f"dq_{n}{i}", sem(f"s_dq_{n}{i}"), 16) for i in range(k)]
            self.dqi[n] = 0

    def _wait(self, eng, reads, writes):
        need = {}
        for b in reads:
            if b.w is not None:
                st, c = b.w
                if need.get(st, 0) < c:
                    need[st] = c
        own_ok = eng is not self.pe
        for b in writes:
            if b.w is not None:
                st, c = b.w
                if (st is not eng or own_ok) and need.get(st, 0) < c:
                    need[st] = c
            for st, c in b.r.items():
                if (st is not eng or own_ok) and need.get(st, 0) < c:
                    need[st] = c
        for st, c in need.items():
            if eng.seen.get(st, 0) < c:
                eng.h.wait_ge(st.sem, c)
                eng.seen[st] = c

    def _mark(self, st, cid, reads, writes):
        for b in reads:
            if b.r.get(st, 0) < cid:
                b.r[st] = cid
        for b in writes:
            b.w = (st, cid)
            b.r = {}

    def op(self, eng, ins_fn, reads=(), writes=()):
        self._wait(eng, reads, writes)
        ins = ins_fn(eng.h)
        eng.count += 1
        ins.then_inc(eng.sem, 1)
        eng.ninst += 1
        self._mark(eng, eng.count, reads, writes)
        return ins

    def mm_group(self, mms, reads, writes):
        eng = self.pe
        self._wait(eng, reads, writes)
        n = len(mms)
        ins = None
        for i, (o, l, r) in enumerate(mms):
            ins = eng.h.matmul(o, l, r, start=(i == 0), stop=(i == n - 1))
            eng.ninst += 1
        eng.count += 1
        ins.then_inc(eng.sem, 1)
        self._mark(eng, eng.count, reads, writes)

    def mm_multi(self, mms, reads, writes):
        eng = self.pe
        self._wait(eng, reads, writes)
        ins = None
        for (o, l, r, st_, sp_) in mms:
            ins = eng.h.matmul(o, l, r, start=st_, stop=sp_)
            eng.ninst += 1
        eng.count += 1
        ins.then_inc(eng.sem, 1)
        self._mark(eng, eng.count, reads, writes)

    def mm(self, o, l, r, start, stop, reads, writes):
        eng = self.pe
        self._wait(eng, reads, writes)
        ins = eng.h.matmul(o, l, r, start=start, stop=stop)
        eng.ninst += 1
        eng.count += 1
        ins.then_inc(eng.sem, 1)
        self._mark(eng, eng.count, reads, writes)

    def dma(self, qeng, dq, out, in_, reads=(), writes=()):
        ring = self.dq[dq]
        st = ring[self.dqi[dq] % len(ring)]
        self.dqi[dq] += 1
        if len(ring) > 1 and st.count > 0 and qeng.seen.get(st, 0) < st.count:
            qeng.h.wait_ge(st.sem, st.count)
            qeng.seen[st] = st.count
        self._wait(qeng, reads, writes)
        ins = qeng.h.dma_start(out=out, in_=in_)
        st.count += 16
        ins.then_inc(st.sem, 16)
        qeng.ninst += 1
        self._mark(st, st.count, reads, writes)
        return ins

    def barrier(self):
        streams = list(self.engines) + [s for r in self.dq.values() for s in r]
        for e in self.engines:
            for st in streams:
                if st is e:
                    continue
                if st.count > 0 and e.seen.get(st, 0) < st.count:
                    e.h.wait_ge(st.sem, st.count)
                    e.seen[st] = st.count


class TPool:
    def __init__(self, kb, name, shape, dtype, n):
        self.tiles = [kb.sb(name, shape, dtype) for _ in range(n)]
        self.bufs = [Buf() for _ in range(n)]
        self.i = 0

    def next(self):
        j = self.i % len(self.tiles)
        self.i += 1
        return self.tiles[j], self.bufs[j]


class KB:
    def __init__(self, nc, depth, plan=None):
        self.nc = nc
        self.depth = depth
        self.plan = plan
        self.uid = 0

    def sb(self, name, shape, dtype):
        self.uid += 1
        return self.cur.enter_context(self.nc.sbuf_tensor(f"{name}_{self.uid}", shape, dtype))

    @contextmanager
    def scope(self):
        prev = self.cur
        with ExitStack() as st:
            self.cur = st
            yield
            self.fw.barrier()
        self.cur = prev

    def psum(self, banks=None):
        banks = banks or self.all_banks
        j = banks[self.ps_i % len(banks)]
        self.ps_i += 1
        return self.ps[j], self.psb[j]

    def din(self, name, shape):
        return self.nc.dram_tensor(name, list(shape), F32, kind="ExternalInput").ap()

    def build(self):
        nc = self.nc
        depth = self.depth
        self.xT_d = self.din("xT", [D, S])
        self.memT_d = self.din("memT", [D, MEM])
        self.vecs_d = self.din("vecs", [128, L * V_PER])
        self.lamb_d = self.din("lamb", [128, L * 256])
        self.bstrip_d = self.din("bstrip", [128, NH * 256])
        self.b31_d = self.din("b31b", [128, NH])
        self.ident_d = self.din("ident", [128, 128])
        self.poolc_d = self.din("poolc", [128, 64])
        self.w1_d = self.din("ffn_w1", [L, 2, D, DFF])
        self.w3_d = self.din("ffn_w3", [L, 2, D, DFF])
        self.w2_d = self.din("ffn_w2", [L, 2, DFF, D])
        self.win_d = self.din("w_in", [L, D, IN_COLS])
        self.cpw_d = self.din("conv_pw", [L, D, D])
        self.ao_d = self.din("attn_o", [L, D, D])
        self.pw_d = self.din("pool_w", [L, 4, 256, 256])
        self.po_d = self.din("pool_o", [L, D, D])
        self.wo_d = self.din("w_out", [L, D, D])
        self.xq_d = self.din("xattn_q", [L, D, D])
        self.xkv_d = self.din("xattn_kv", [L, D, 2 * D])
        self.xo_d = self.din("xattn_o", [L, D, D])
        self.out_d = nc.dram_tensor("outT", [D, S], F32, kind="ExternalOutput").ap()

        with ExitStack() as ctx:
            self.cur = ctx
            self.fw = fw = FW(nc, ctx)
            self.ps = [ctx.enter_context(nc.psum_tensor(f"ps{i}", [128, 512], F32)) for i in range(8)]
            self.psb = [Buf() for _ in range(8)]
            self.all_banks = list(range(8))
            self.ps_i = 0
            self.x = self.sb("x", [128, KD, S], F32)
            self.xb = [[Buf() for _ in range(4)] for _ in range(KD)]
            self.vecs = self.sb("vecs", [128, V_PER], F32)
            self.vecs_b = Buf()
            self.gh = self.sb("gh", [128, 64], F32)
            self.ones1f = self.sb("ones1f", [128, 128], F32)
            self.rx = self.sb("rx", [128, S], F32)
            self.rxb = [Buf() for _ in range(4)]
            self.cur_layer = -1
            self.gh_b = Buf()
            self.onesD = self.sb("onesD", [128, 128], BF16)
            self.ones128 = self.sb("ones128", [128, 128], BF16)
            self.ones1 = self.sb("ones1", [128, 128], BF16)
            self.identf = self.sb("identf", [128, 128], F32)
            self.identb = self.sb("identb", [128, 128], BF16)
            self.cb = Buf()
            self.epsr = self.sb("epsr", [128, 1], F32)
            self.epsl = self.sb("epsl", [128, 1], F32)

            fw.dma(fw.sp, "a", self.identf[:], self.ident_d, writes=[self.cb])
            xv = self.xT_d.rearrange("(c p) t -> p c t", p=128)
            for c in range(KD):
                fw.dma(fw.sp, "a", self.x[:, c, :], xv[:, c, :], writes=self.xb[c])
            fw.op(fw.dve, lambda h: h.memset(self.onesD[:], 1.0 / D), writes=[self.cb])
            fw.op(fw.dve, lambda h: h.memset(self.ones128[:], 1.0 / 128), writes=[self.cb])
            fw.op(fw.dve, lambda h: h.memset(self.ones1[:], 1.0), writes=[self.cb])
            fw.op(fw.dve, lambda h: h.memset(self.epsr[:], RMS_EPS), writes=[self.cb])
            fw.op(fw.dve, lambda h: h.memset(self.epsl[:], LN_EPS), writes=[self.cb])
            fw.op(fw.dve, lambda h: h.tensor_copy(self.identb[:], self.identf[:]), reads=[self.cb], writes=[self.cb])
            fw.op(fw.dve, lambda h: h.memset(self.ones1f[:], 1.0), writes=[self.cb])
            fw.barrier()

            with self.scope():
                sqpool = TPool(self, "sq", [128, 512], BF16, 3)
                rtpool = TPool(self, "rt", [128, 512], F32, 2)
                for tb in range(4):
                    self.update_rstd(tb, sqpool, rtpool)

            plan = self.plan or [(l, ph) for l in range(depth) for ph in ("ffn0", "mixer", "xattn", "ffn1")]
            for (l, ph) in plan:
                if l != self.cur_layer:
                    self.cur_layer = l
                    fw.barrier()
                    fw.dma(fw.sp, "a", self.vecs[:], self.vecs_d[:, l * V_PER:(l + 1) * V_PER], writes=[self.vecs_b])
                    fw.op(fw.dve, lambda h: h.tensor_scalar(self.gh[:], self.vecs[:, V_NG:V_NG + 64], 0.5, None, ALU.mult),
                          reads=[self.vecs_b], writes=[self.gh_b])
                getattr(self, "ph_" + ph)(l)

            fw.barrier()
            ov = self.out_d.rearrange("(c p) t -> p c t", p=128)
            for c in range(KD):
                fw.dma(fw.sp, "o", ov[:, c, :], self.x[:, c, :], reads=self.xb[c])
            st = fw.dq["o"][0]
            fw.sp.h.wait_ge(st.sem, st.count)
            self.stats = {e.name: (e.ninst, e.count) for e in fw.engines}

    def gcol(self, l, n, c):
        o = V_NG + n * 8 + c
        return self.vecs[:, o:o + 1]

    def ghcol(self, l, n, c):
        o = n * 8 + c
        return self.gh[:, o:o + 1]

    def vcol(self, l, off, c):
        o = off + c
        return self.vecs[:, o:o + 1]

    def rstd_blocks(self, srcs, nblk, rstd, rstd_b, ones, eps_tile, sqpool, rtpool, W=512, sq_on_dve=False, banks=None):
        fw = self.fw
        n = len(srcs)
        for tb in range(nblk):
            ps, pb = self.psum(banks)
            for c, (apf, bf) in enumerate(srcs):
                sq, sqb = sqpool.next()
                if sq_on_dve:
                    fw.op(fw.dve, lambda h: h.tensor_tensor(sq[:, :W], apf(tb), apf(tb), ALU.mult), reads=bf(tb), writes=[sqb])
                else:
                    fw.op(fw.act, lambda h: h.activation(sq[:, :W], apf(tb), AF.Square), reads=bf(tb), writes=[sqb])
                fw.mm(ps[:, :W], ones, sq[:, :W], c == 0, c == n - 1, reads=[sqb, self.cb], writes=[pb])
            rt, rtb = rtpool.next()
            fw.op(fw.act, lambda h: h.activation(rt[:, :W], ps[:, :W], AF.Ln, bias=eps_tile, scale=1.0),
                  reads=[pb, self.cb], writes=[rtb])
            fw.op(fw.act, lambda h: h.activation(rstd[:, tb * W:(tb + 1) * W], rt[:, :W], AF.Exp, scale=-0.5),
                  reads=[rtb], writes=[rstd_b])

    def update_rstd(self, tb, sqpool, rtpool, sq_on_dve=False, banks=None):
        ts = slice(tb * 512, (tb + 1) * 512)
        srcs = [(lambda t_, c=c: self.x[:, c, ts], lambda t_, c=c: [self.xb[c][tb]]) for c in range(KD)]
        self.rstd_blocks(srcs, 1, self.rx[:, ts], self.rxb[tb], self.onesD[:], self.epsr[:], sqpool, rtpool,
                         sq_on_dve=sq_on_dve, banks=banks)

    def recip(self, out, in_, reads, writes):
        if FAST_RECIP:
            self.fw.op(self.fw.dve, lambda h: h.reciprocal_approx_fast(out, in_), reads=reads, writes=writes)
        else:
            self.fw.op(self.fw.dve, lambda h: h.reciprocal(out, in_), reads=reads, writes=writes)

    def wload_cols(self, pool, src2d, nblk):
        t, _ = pool.next()
        ncol = src2d.shape[1] // nblk
        bufs = []
        for i in range(nblk):
            b = Buf()
            self.fw.dma(self.fw.pool, "w", t[:, :, i * ncol:(i + 1) * ncol],
                        src2d[:, i * ncol:(i + 1) * ncol].rearrange("(k p) n -> p k n", p=128), writes=[b])
            bufs.append(b)
        return t, bufs

    def wload(self, pool, src2d):
        t, b = pool.next()
        self.fw.dma(self.fw.pool, "w", t[:], src2d.rearrange("(k p) n -> p k n", p=128), writes=[b])
        return t, b

    def ph_ffn0(self, l):
        self.ffn(l, 0, 0, 1)

    def ph_ffn1(self, l):
        self.ffn(l, 1, 6, 7)

    def ffn(self, l, i, n_in, n_out):
        fw = self.fw
        T = 1024
        w1 = self.w1_d[l, i]
        w3 = self.w3_d[l, i]
        w2 = self.w2_d[l, i]
        with self.scope():
            s = self.sb("s", [128, KF, T], BF16)
            sb_ = [[Buf() for _ in range(2)] for _ in range(KF)]
            w13pool = TPool(self, "w13", [128, KD, 256], BF16, 4)
            w2pool = TPool(self, "w2", [128, KF, 256], BF16, 2)
            pre_w13 = None
            pre_w2 = None
            for half in range(2):
                t0 = half * T
                with self.scope():
                    hT = self.sb("hT", [128, KD, T], BF16)
                    hb = [[Buf() for _ in range(2)] for _ in range(KD)]
                    sapool = TPool(self, "sa", [128, 512], F32, 3)
                    for tb in range(2):
                        gtb = half * 2 + tb
                        gts = slice(gtb * 512, (gtb + 1) * 512)
                        for c in range(KD):
                            fw.op(fw.dve, lambda h: h.scalar_tensor_tensor(
                                hT[:, c, tb * 512:(tb + 1) * 512], self.x[:, c, gts], self.gcol(l, n_in, c), self.rx[:, gts],
                                ALU.mult, ALU.mult),
                                reads=[self.xb[c][gtb], self.rxb[gtb], self.vecs_b], writes=[hb[c][tb]])
                    hall = [b for c in range(KD) for b in hb[c]]
                    for mg in range(KF // 2):
                        if mg == 0 and pre_w13 is not None:
                            (w1t, w1b), (w3t, w3b) = pre_w13
                            pre_w13 = None
                        else:
                            w1t, w1b = self.wload(w13pool, w1[:, mg * 256:(mg + 1) * 256])
                            w3t, w3b = self.wload(w13pool, w3[:, mg * 256:(mg + 1) * 256])
                        if mg == KF // 2 - 2:
                            pre_w2 = self.wload(w2pool, w2[:, 0:256])
                        for mm_ in range(2):
                            m = mg * 2 + mm_
                            for tb in range(2):
                                hr = [hb[c][tb] for c in range(KD)]
                                pa, pab = self.psum()
                                fw.mm_group([(pa[:], w1t[:, k, mm_ * 128:(mm_ + 1) * 128],
                                              hT[:, k, tb * 512:(tb + 1) * 512]) for k in range(KD)],
                                            reads=[w1b] + hr, writes=[pab])
                                pbk, pbb = self.psum()
                                fw.mm_group([(pbk[:], w3t[:, k, mm_ * 128:(mm_ + 1) * 128],
                                              hT[:, k, tb * 512:(tb + 1) * 512]) for k in range(KD)],
                                            reads=[w3b] + hr, writes=[pbb])
                                sa, sab = sapool.next()
                                fw.op(fw.act, lambda h: h.activation(sa[:], pa[:], AF.Silu),
                                      reads=[pab], writes=[sab])
                                fw.op(fw.dve, lambda h: h.tensor_tensor(
                                    s[:, m, tb * 512:(tb + 1) * 512], sa[:], pbk[:], ALU.mult),
                                    reads=[sab, pbb], writes=[sb_[m][tb]])
                with self.scope():
                    y = self.sb("y", [128, KD, T], F32)
                    yb = [[Buf() for _ in range(2)] for _ in range(KD)]
                    rstd = self.sb("rstd", [128, T], F32)
                    rstd_b = Buf()
                    sqpool = TPool(self, "sq", [128, 512], BF16, 4)
                    rtpool = TPool(self, "rt", [128, 512], F32, 2)
                    tmppool = TPool(self, "tmp", [128, 512], F32, 2)
                    B06 = [0, 1, 2, 3, 4, 5]
                    stat_ps = [(self.ps[6], self.psb[6]), (self.ps[7], self.psb[7])]
                    pend = []

                    def stat_mm(tb_, mo_, sq_, sqb_):
                        fw.mm(stat_ps[tb_][0][:], self.onesD[:], sq_[:], mo_ == 0, mo_ == KD - 1,
                              reads=[sqb_, self.cb], writes=[stat_ps[tb_][1]])
                    for mo2 in range(4):
                        if mo2 == 0:
                            w2t, w2b = pre_w2
                        else:
                            w2t, w2b = self.wload(w2pool, w2[:, mo2 * 256:(mo2 + 1) * 256])
                        if mo2 == 2 and half == 0:
                            pre_w13 = (self.wload(w13pool, w1[:, 0:256]), self.wload(w13pool, w3[:, 0:256]))
                        for mm_ in range(2):
                            mo = mo2 * 2 + mm_
                            for tb in range(2):
                                p, pb = self.psum(B06)
                                fw.mm_group([(p[:], w2t[:, k, mm_ * 128:(mm_ + 1) * 128],
                                              s[:, k, tb * 512:(tb + 1) * 512]) for k in range(KF)],
                                            reads=[w2b] + [sb_[k][tb] for k in range(KF)], writes=[pb])
                                fw.op(fw.act, lambda h: h.copy(y[:, mo, tb * 512:(tb + 1) * 512], p[:]),
                                      reads=[pb], writes=[yb[mo][tb]])
                                sq, sqb = sqpool.next()
                                fw.op(fw.act, lambda h: h.activation(sq[:], p[:], AF.Square), reads=[pb], writes=[sqb])
                                pend.append((tb, mo, sq, sqb))
                                if len(pend) > 2:
                                    stat_mm(*pend.pop(0))
                    while pend:
                        stat_mm(*pend.pop(0))
                    def epi1(tb):
                        rb_ = Buf()
                        rt, rtb = rtpool.next()
                        fw.op(fw.act, lambda h: h.activation(rt[:], stat_ps[tb][0][:], AF.Ln, bias=self.epsr[:], scale=1.0),
                              reads=[stat_ps[tb][1], self.cb], writes=[rtb])
                        fw.op(fw.act, lambda h: h.activation(rstd[:, tb * 512:(tb + 1) * 512], rt[:], AF.Exp, scale=-0.5),
                              reads=[rtb], writes=[rb_])
                        return rb_

                    def epi2(tb, rb_):
                        gtb = half * 2 + tb
                        gts = slice(gtb * 512, (gtb + 1) * 512)
                        for c in range(KD):
                            tmp, tmpb = tmppool.next()
                            fw.op(fw.dve, lambda h: h.scalar_tensor_tensor(
                                tmp[:], y[:, c, tb * 512:(tb + 1) * 512], self.ghcol(l, n_out, c), rstd[:, tb * 512:(tb + 1) * 512],
                                ALU.mult, ALU.mult),
                                reads=[yb[c][tb], rb_, self.gh_b], writes=[tmpb])
                            fw.op(fw.dve, lambda h: h.tensor_tensor(
                                self.x[:, c, gts], self.x[:, c, gts], tmp[:], ALU.add),
                                reads=[tmpb, self.xb[c][gtb]], writes=[self.xb[c][gtb]])

                    last = (self.plan is None and l == self.depth - 1 and i == 1)
                    r0 = epi1(0)
                    r1 = epi1(1)
                    epi2(0, r0)
                    if not last:
                        self.update_rstd(half * 2, sqpool, rtpool)
                    epi2(1, r1)
                    if not last:
                        self.update_rstd(half * 2 + 1, sqpool, rtpool)

    def ph_mixer(self, l):
        fw = self.fw
        win = self.win_d[l]
        lam_init = 0.8 - 0.6 * math.exp(-0.3 * l)
        with self.scope():
            R_ = self.sb("R", [128, KD, S], BF16)
            rb = [[Buf() for _ in range(4)] for _ in range(KD)]
            with self.scope():
                hT = self.sb("hTm", [128, KD, S], BF16)
                hb = [[Buf() for _ in range(4)] for _ in range(KD)]
                self.norm_full(l, 2, hT, hb)
                self.attn_core(l, win, hT, hb, R_, rb, lam_init)
                self.attn_proj_inplace(l, win, hT, hb, R_, rb)
                self.conv_branch(l, win, hT, hb, R_, rb)
                self.pool_branch(l, win, hT, hb, R_, rb)
            self.proj_norm_res(l, self.wo_d[l], R_, rb, 3)

    def attn_core(self, l, win, hT, hb, oT, ob, lam_init):
        fw = self.fw
        B6 = [4, 5, 6, 7]
        with self.scope():
            bst = self.sb("bst", [128, NH, 256], F32)
            bstb = Buf()
            b31 = self.sb("b31", [128, NH], F32)
            lamt = self.sb("lamt", [128, 256], F32)
            junk = self.sb("junk", [128, 64], F32)
            s12 = self.sb("s12", [128, 2], F32)
            e12 = self.sb("e12", [128, 2], F32)
            neglam = self.sb("neglam", [128, 1], F32)
            subS = self.sb("subS", [128, 1], F32)
            smb = Buf()
            fw.dma(fw.sp, "a", bst[:], self.bstrip_d.rearrange("p (h j) -> p h j", h=NH), writes=[bstb])
            fw.dma(fw.sp, "a", b31[:], self.b31_d, writes=[bstb])
            fw.dma(fw.sp, "a", lamt[:], self.lamb_d[:, l * 256:(l + 1) * 256], writes=[smb])
            for h_ in range(NH):
                fw.op(fw.dve, lambda h: h.tensor_scalar(bst[:, h_, :], bst[:, h_, :], b31[:, h_:h_ + 1], None, ALU.subtract),
                      reads=[bstb], writes=[bstb])
            fw.op(fw.dve, lambda h: h.scalar_tensor_tensor(junk[:], lamt[:, 0:64], 1.0, lamt[:, 64:128], ALU.mult, ALU.mult,
                                                           accum_out=s12[:, 0:1]), reads=[smb], writes=[smb])
            fw.op(fw.dve, lambda h: h.scalar_tensor_tensor(junk[:], lamt[:, 128:192], 1.0, lamt[:, 192:256], ALU.mult, ALU.mult,
                                                           accum_out=s12[:, 1:2]), reads=[smb], writes=[smb])
            fw.op(fw.act, lambda h: h.activation(e12[:], s12[:], AF.Exp), reads=[smb], writes=[smb])
            fw.op(fw.dve, lambda h: h.tensor_tensor(neglam[:], e12[:, 1:2], e12[:, 0:1], ALU.subtract), reads=[smb], writes=[smb])
            fw.op(fw.dve, lambda h: h.tensor_scalar(neglam[:], neglam[:], -lam_init, None, ALU.add), reads=[smb], writes=[smb])
            fw.op(fw.dve, lambda h: h.tensor_scalar(subS[:], self.vcol(l, V_SUB, 0), 1.0 - lam_init, None, ALU.mult),
                  reads=[smb, self.vecs_b], writes=[smb])
            wpool = TPool(self, "wqkv", [128, KD, 256], BF16, 3)
            vpool = TPool(self, "Vt", [128, 16, 256], BF16, 1)
            qpool = TPool(self, "qT", [128, S], BF16, 2)
            kpool = TPool(self, "kT", [128, S], BF16, 2)
            epool = TPool(self, "E", [128, 512], BF16, 8)
            fpool = TPool(self, "fin", [128, 512], F32, 6)
            sqpool = TPool(self, "sq", [128, 512], BF16, 2)
            acc = [self.ps[i] for i in range(4)]
            accb = [self.psb[i] for i in range(4)]
            defer = []

            def drain(n):
                for _ in range(min(n, len(defer))):
                    defer.pop(0)()
            def proj_v(hp):
                wv, wvb = self.wload(wpool, win[:, 4096 + hp * 256:4096 + (hp + 1) * 256])
                V, Vb = vpool.next()
                for tk in range(16):
                    p, pb = self.psum(B6)
                    fw.mm_group([(p[:, :256], hT[:, k, tk * 128:(tk + 1) * 128], wv[:, k, :]) for k in range(KD)],
                                reads=[wvb] + [hb[k][tk // 4] for k in range(KD)], writes=[pb])
                    fw.op(fw.dve, lambda h: h.tensor_copy(V[:, tk, :], p[:, :256]), reads=[pb], writes=[Vb])
                return V, Vb

            def qk_tasks(e, wq, wqb, wk, wkb, qT, qb, kT, kb_):
                ts_ = []
                for tb in range(4):
                    def tq(tb=tb):
                        ts = slice(tb * 512, (tb + 1) * 512)
                        p, pb = self.psum(B6)
                        fw.mm_group([(p[:], wq[:, k, e * 128:(e + 1) * 128], hT[:, k, ts]) for k in range(KD)],
                                    reads=[wqb] + [hb[k][tb] for k in range(KD)], writes=[pb])
                        fw.op(fw.dve, lambda h: h.tensor_scalar(qT[:, ts], p[:], 0.125, None, ALU.mult), reads=[pb], writes=[qb])

                    def tk_(tb=tb):
                        ts = slice(tb * 512, (tb + 1) * 512)
                        p, pb = self.psum(B6)
                        fw.mm_group([(p[:], wk[:, k, e * 128:(e + 1) * 128], hT[:, k, ts]) for k in range(KD)],
                                    reads=[wkb] + [hb[k][tb] for k in range(KD)], writes=[pb])
                        fw.op(fw.dve, lambda h: h.tensor_copy(kT[:, ts], p[:]), reads=[pb], writes=[kb_])
                    ts_ += [tq, tk_]
                return ts_

            def run_head(hd, e, V, Vb, qT, qb, kT, kb_, tasks):
                units = [(I, j) for I in range(4) for j in range(4 * I + 4)]

                def emit_scores(I, j):
                    r = j - 4 * I
                    qlo = max(r, 0) * 128
                    N = 512 - qlo
                    sps = [self.psum(B6), self.psum(B6)]
                    fw.mm_multi([(sps[m][0][:, :N], kT[m * 64:(m + 1) * 64, j * 128:(j + 1) * 128],
                                  qT[m * 64:(m + 1) * 64, I * 512 + qlo:(I + 1) * 512], True, True) for m in range(2)],
                                reads=[kb_, qb], writes=[sps[0][1], sps[1][1]])
                    es = []
                    for m in range(2):
                        sp_, spb = sps[m]
                        if r >= -1:
                            off, nn = (0, min(256, N)) if r >= 0 else (128, 128)
                            fw.op(fw.dve, lambda h: h.tensor_tensor(sp_[:, :nn], sp_[:, :nn], bst[:, hd, off:off + nn], ALU.add),
                                  reads=[spb, bstb], writes=[spb])
                        e_, eb = epool.next()
                        fw.op(fw.act, lambda h: h.activation(e_[:, :N], sp_[:, :N], AF.Exp), reads=[spb], writes=[eb])
                        es.append((e_, eb))
                    return es, qlo, N

                pending = [emit_scores(*units[0]), emit_scores(*units[1])]
                for ui, (I, j) in enumerate(units):
                    es, qlo, N = pending.pop(0)
                    if ui + 2 < len(units):
                        pending.append(emit_scores(*units[ui + 2]))
                    nkb = 4 * I + 4
                    mms = []
                    for m in range(2):
                        mms.append((acc[m][:, qlo:512], V[:, j, e * 128:(e + 1) * 128], es[m][0][:, :N], j == 0, j == nkb - 1))
                        mms.append((acc[2 + m][:, qlo:512], self.ones1[:], es[m][0][:, :N], j == 0, j == nkb - 1))
                    fw.mm_multi(mms, reads=[Vb, self.cb, es[0][1], es[1][1]], writes=accb)
                    drain(2)
                    if tasks and ui % 4 == 3:
                        tasks.pop(0)()
                    if j != nkb - 1:
                        continue
                    drain(len(defer))
                    self.attn_finalize(defer, acc, accb, fpool, sqpool, B6, neglam, subS, smb, oT, ob, hd, I)

            V, Vb = proj_v(0)
            wq, wqb = self.wload(wpool, win[:, 2048:2048 + 256])
            wk, wkb = self.wload(wpool, win[:, 3072:3072 + 256])
            qT, qb = qpool.next()
            kT, kb_ = kpool.next()
            for t_ in qk_tasks(0, wq, wqb, wk, wkb, qT, qb, kT, kb_):
                t_()
            for hd in range(NH):
                hp, e = divmod(hd, 2)
                tasks = []
                nxt_qk = None
                if hd + 1 < NH:
                    nhp, ne = divmod(hd + 1, 2)
                    if ne == 0:
                        wq, wqb = self.wload(wpool, win[:, 2048 + nhp * 256:2048 + (nhp + 1) * 256])
                        wk, wkb = self.wload(wpool, win[:, 3072 + nhp * 256:3072 + (nhp + 1) * 256])
                    nq, nqb = qpool.next()
                    nk, nkb_ = kpool.next()
                    tasks = qk_tasks(ne, wq, wqb, wk, wkb, nq, nqb, nk, nkb_)
                    nxt_qk = (nq, nqb, nk, nkb_)
                run_head(hd, e, V, Vb, qT, qb, kT, kb_, tasks)
                while tasks:
                    tasks.pop(0)()
                if e == 1 and hd + 1 < NH:
                    V, Vb = proj_v(hp + 1)
                if nxt_qk is not None:
                    qT, qb, kT, kb_ = nxt_qk
            drain(len(defer))

    def attn_finalize(self, defer, acc, accb, fpool, sqpool, B6, neglam, subS, smb, oT, ob, hd, I):
        fw = self.fw
        z0, z0b = fpool.next()
        fw.op(fw.act, lambda h: h.copy(z0[:], acc[2][:]), reads=[accb[2]], writes=[z0b])
        z1, z1b = fpool.next()
        fw.op(fw.act, lambda h: h.copy(z1[:], acc[3][:]), reads=[accb[3]], writes=[z1b])
        t0, t0b = fpool.next()
        fw.op(fw.dve, lambda h: h.tensor_copy(t0[:], acc[0][:]), reads=[accb[0]], writes=[t0b])
        t1, t1b = fpool.next()
        fw.op(fw.dve, lambda h: h.tensor_copy(t1[:], acc[1][:]), reads=[accb[1]], writes=[t1b])
        st = {}

        def s1():
            fw.op(fw.dve, lambda h: h.tensor_tensor(t0[:], t0[:], z1[:], ALU.mult), reads=[t0b, z1b], writes=[t0b])

        def s2():
            fw.op(fw.dve, lambda h: h.tensor_tensor(t1[:], t1[:], z0[:], ALU.mult), reads=[t1b, z0b], writes=[t1b])

        def s3():
            fw.op(fw.dve, lambda h: h.scalar_tensor_tensor(t0[:], t1[:], neglam[:, 0:1], t0[:], ALU.mult, ALU.add),
                  reads=[t0b, t1b, smb], writes=[t0b])

        def s4():
            sq, sqb = sqpool.next()
            fw.op(fw.act, lambda h: h.activation(sq[:], t0[:], AF.Square), reads=[t0b], writes=[sqb])
            st["sq"] = (sq, sqb)

        def s5():
            sq, sqb = st["sq"]
            pst, pstb = self.psum(B6)
            fw.mm(pst[:], self.ones128[:], sq[:], True, True, reads=[sqb, self.cb], writes=[pstb])
            st["pst"] = (pst, pstb)

        def s6():
            fw.op(fw.dve, lambda h: h.tensor_tensor(z0[:], z0[:], z1[:], ALU.mult), reads=[z0b, z1b], writes=[z0b])

        def s7():
            fw.op(fw.dve, lambda h: h.scalar_tensor_tensor(z0[:], z0[:], RMS_EPS, z0[:], ALU.mult, ALU.mult),
                  reads=[z0b], writes=[z0b])

        def s8():
            pst, pstb = st["pst"]
            fw.op(fw.dve, lambda h: h.tensor_tensor(z0[:], pst[:], z0[:], ALU.add), reads=[pstb, z0b], writes=[z0b])

        def s9():
            fw.op(fw.act, lambda h: h.activation(z0[:], z0[:], AF.Ln), reads=[z0b], writes=[z0b])

        def s10():
            fw.op(fw.act, lambda h: h.activation(z0[:], z0[:], AF.Exp, scale=-0.5), reads=[z0b], writes=[z0b])

        def s11():
            fw.op(fw.dve, lambda h: h.scalar_tensor_tensor(oT[:, hd, I * 512:(I + 1) * 512], t0[:], subS[:, 0:1], z0[:],
                                                           ALU.mult, ALU.mult),
                  reads=[t0b, z0b, smb], writes=[ob[hd][I]])

        defer.extend([s1, s2, s3, s4, s6, s7, s5, s8, s9, s10, s11])

    def attn_proj_inplace(self, l, win, hT, hb, R_, rb):
        fw = self.fw
        goff = 6144 + D
        with self.scope():
            wpool = TPool(self, "wres2", [128, KD, D], BF16, 2)
            mt = self.sb("mtmp", [128, KD, 512], BF16)
            mtb = Buf()
            sgpool = TPool(self, "sg", [128, 512], F32, 3)
            wa, _ = wpool.next()
            wg, _ = wpool.next()
            wabs, wgbs = [], []
            for i in range(4):
                for (t_, s_, bl) in ((wa, self.ao_d[l], wabs), (wg, win[:, goff:goff + D], wgbs)):
                    b = Buf()
                    fw.dma(fw.pool, "w", t_[:, :, i * 256:(i + 1) * 256],
                           s_[:, i * 256:(i + 1) * 256].rearrange("(k p) n -> p k n", p=128), writes=[b])
                    bl.append(b)
            for tb in range(4):
                ts = slice(tb * 512, (tb + 1) * 512)
                for mo in range(KD):
                    pa, pab = self.psum()
                    fw.mm_group([(pa[:], wa[:, k, mo * 128:(mo + 1) * 128], R_[:, k, ts]) for k in range(KD)],
                                reads=[wabs[mo // 2]] + [rb[k][tb] for k in range(KD)], writes=[pab])
                    pg, pgb = self.psum()
                    fw.mm_group([(pg[:], wg[:, k, mo * 128:(mo + 1) * 128], hT[:, k, ts]) for k in range(KD)],
                                reads=[wgbs[mo // 2]] + [hb[k][tb] for k in range(KD)], writes=[pgb])
                    sg, sgb = sgpool.next()
                    fw.op(fw.act, lambda h: h.activation(sg[:], pg[:], AF.Sigmoid), reads=[pgb], writes=[sgb])
                    fw.op(fw.dve, lambda h: h.tensor_tensor(mt[:, mo, :], pa[:], sg[:], ALU.mult), reads=[pab, sgb], writes=[mtb])
                fw.op(fw.dve, lambda h: h.tensor_copy(R_[:, :, ts], mt[:]), reads=[mtb], writes=[rb[k][tb] for k in range(KD)])

    def gate_accum(self, l, w2d, inT, inb, goff, win, hT, hb, R_, rb, wpool, sgpool):
        fw = self.fw
        for mo2 in range(4):
            wa, wab = self.wload(wpool, w2d[:, mo2 * 256:(mo2 + 1) * 256])
            wg, wgb = self.wload(wpool, win[:, goff + mo2 * 256:goff + (mo2 + 1) * 256])
            for mm_ in range(2):
                mo = mo2 * 2 + mm_
                for q in range(4):
                    qs = slice(q * 512, (q + 1) * 512)
                    pa, pab = self.psum()
                    fw.mm_group([(pa[:], wa[:, k, mm_ * 128:(mm_ + 1) * 128], inT[:, k, qs]) for k in range(KD)],
                                reads=[wab] + [inb(k, q) for k in range(KD)], writes=[pab])
                    pg, pgb = self.psum()
                    fw.mm_group([(pg[:], wg[:, k, mm_ * 128:(mm_ + 1) * 128], hT[:, k, qs]) for k in range(KD)],
                                reads=[wgb] + [hb[k][q] for k in range(KD)], writes=[pgb])
                    sg, sgb = sgpool.next()
                    fw.op(fw.act, lambda h: h.activation(sg[:], pg[:], AF.Sigmoid), reads=[pgb], writes=[sgb])
                    fw.op(fw.dve, lambda h: h.tensor_tensor(sg[:], pa[:], sg[:], ALU.mult), reads=[pab, sgb], writes=[sgb])
                    fw.op(fw.dve, lambda h: h.tensor_tensor(R_[:, mo, qs], R_[:, mo, qs], sg[:], ALU.add),
                          reads=[sgb, rb[mo][q]], writes=[rb[mo][q]])

    def conv_branch(self, l, win, hT, hb, R_, rb):
        fw = self.fw
        with self.scope():
            zc = self.sb("zc", [128, KD, S], BF16)
            zcb = [[Buf() for _ in range(4)] for _ in range(KD)]
            wpool = TPool(self, "wcv", [128, KD, 256], BF16, 3)
            dgpool = TPool(self, "diag", [128, CK, 128], BF16, 1)
            zpool = TPool(self, "z", [128, 30 + 512], BF16, 3)
            zhalo = self.sb("zhalo", [128, 2, 30], BF16)
            zhb = [Buf() for _ in range(2)]
            sgpool = TPool(self, "sg", [128, 512], F32, 2)
            sqpool = TPool(self, "sq", [128, 512], BF16, 3)
            stpool = TPool(self, "st", [128, 512], F32, 3)
            for cp in range(4):
                wa, wab = self.wload(wpool, win[:, cp * 256:(cp + 1) * 256])
                wg, wgb = self.wload(wpool, win[:, D + cp * 256:D + (cp + 1) * 256])
                for cc in range(2):
                    c = cp * 2 + cc
                    dg, dgb = dgpool.next()
                    o = V_CDW + c * CK
                    fw.op(fw.dve, lambda h: h.tensor_tensor(
                        dg[:], self.identb[:].unsqueeze(1).to_broadcast([128, CK, 128]),
                        self.vecs[:, o:o + CK].unsqueeze(2).to_broadcast([128, CK, 128]), ALU.mult),
                        reads=[self.cb, self.vecs_b], writes=[dgb])
                    def produce(q):
                        qs = slice(q * 512, (q + 1) * 512)
                        hr = [hb[k][q] for k in range(KD)]
                        z, zb = zpool.next()
                        if q == 0:
                            fw.op(fw.dve, lambda h: h.memset(z[:, 0:30], 0.0), writes=[zb])
                        else:
                            fw.op(fw.dve, lambda h: h.tensor_copy(z[:, 0:30], zhalo[:, cc, :]), reads=[zhb[cc]], writes=[zb])
                        pa, pab = self.psum()
                        fw.mm_group([(pa[:], wa[:, k, cc * 128:(cc + 1) * 128], hT[:, k, qs]) for k in range(KD)],
                                    reads=[wab] + hr, writes=[pab])
                        pg, pgb = self.psum()
                        fw.mm_group([(pg[:], wg[:, k, cc * 128:(cc + 1) * 128], hT[:, k, qs]) for k in range(KD)],
                                    reads=[wgb] + hr, writes=[pgb])
                        sg, sgb = sgpool.next()
                        fw.op(fw.act, lambda h: h.activation(sg[:], pg[:], AF.Sigmoid), reads=[pgb], writes=[sgb])
                        fw.op(fw.dve, lambda h: h.tensor_tensor(z[:, 30:542], pa[:], sg[:], ALU.mult), reads=[pab, sgb], writes=[zb])
                        if q < 3:
                            fw.op(fw.dve, lambda h: h.tensor_copy(zhalo[:, cc, :], z[:, 512:542]), reads=[zb], writes=[zhb[cc]])
                        return z, zb

                    def consume(q, z, zb):
                        qs = slice(q * 512, (q + 1) * 512)
                        pc, pcb = self.psum()
                        fw.mm_group([(pc[:], dg[:, k, :], z[:, k:k + 512]) for k in range(CK)], reads=[dgb, zb], writes=[pcb])
                        fw.op(fw.act, lambda h: h.activation(zc[:, c, qs], pc[:], AF.Identity, bias=self.vcol(l, V_CDB, c), scale=1.0),
                              reads=[pcb, self.vecs_b], writes=[zcb[c][q]])

                    nz = produce(0)
                    for q in range(4):
                        cz = nz
                        if q < 3:
                            nz = produce(q + 1)
                        consume(q, *cz)
            for q in range(4):
                qs = slice(q * 512, (q + 1) * 512)
                pm, pmb = self.psum()
                pv, pvb = self.psum()
                for c in range(KD):
                    fw.mm(pm[:], self.onesD[:], zc[:, c, qs], c == 0, c == KD - 1, reads=[zcb[c][q], self.cb], writes=[pmb])
                    zsq, zsqb = sqpool.next()
                    fw.op(fw.act, lambda h: h.activation(zsq[:], zc[:, c, qs], AF.Square), reads=[zcb[c][q]], writes=[zsqb])
                    fw.mm(pv[:], self.onesD[:], zsq[:], c == 0, c == KD - 1, reads=[zsqb, self.cb], writes=[pvb])
                mean, meanb = stpool.next()
                fw.op(fw.act, lambda h: h.copy(mean[:], pm[:]), reads=[pmb], writes=[meanb])
                var, varb = stpool.next()
                fw.op(fw.dve, lambda h: h.tensor_tensor(var[:], mean[:], mean[:], ALU.mult), reads=[meanb], writes=[varb])
                fw.op(fw.dve, lambda h: h.tensor_tensor(var[:], pv[:], var[:], ALU.subtract), reads=[pvb, varb], writes=[varb])
                fw.op(fw.act, lambda h: h.activation(var[:], var[:], AF.Sqrt, bias=self.epsl[:], scale=1.0), reads=[varb, self.cb], writes=[varb])
                self.recip(var[:], var[:], [varb], [varb])
                for c in range(KD):
                    t, tb_ = sgpool.next()
                    fw.op(fw.dve, lambda h: h.tensor_tensor(t[:], zc[:, c, qs], mean[:], ALU.subtract), reads=[zcb[c][q], meanb], writes=[tb_])
                    fw.op(fw.dve, lambda h: h.scalar_tensor_tensor(t[:], t[:], self.vcol(l, V_LNG, c), var[:], ALU.mult, ALU.mult),
                          reads=[tb_, varb, self.vecs_b], writes=[tb_])
                    fw.op(fw.act, lambda h: h.activation(zc[:, c, qs], t[:], AF.Silu, bias=self.vcol(l, V_LNB, c), scale=1.0),
                          reads=[tb_, self.vecs_b], writes=[zcb[c][q]])
            self.gate_accum(l, self.cpw_d[l], zc, lambda k, q: zcb[k][q], 6144, win, hT, hb, R_, rb, wpool, sgpool)

    def pool_branch(self, l, win, hT, hb, R_, rb):
        fw = self.fw
        HL = 15
        with self.scope():
            y1 = self.sb("y1", [128, KD, S], BF16)
            y1b = [[Buf() for _ in range(4)] for _ in range(KD)]
            wpool = TPool(self, "wpl", [128, KD, 256], BF16, 3)
            pwpool = TPool(self, "pwt", [128, 2, 256], BF16, 2)
            upool = TPool(self, "u", [128, HL + 512], F32, 3)
            spool = TPool(self, "ws", [128, HL + 512], F32, 3)
            for t_, b_ in zip(spool.tiles, spool.bufs):
                fw.op(fw.dve, lambda h: h.memset(t_[:, 0:HL], 0.0), writes=[b_])
            uhalo = self.sb("uhalo", [128, 2, HL], F32)
            uhb = [Buf() for _ in range(2)]
            plpool = TPool(self, "pooled", [128, 2, 512], BF16, 2)
            sgpool = TPool(self, "sg", [128, 512], F32, 2)
            fx = self.sb("fx", [128, 16], F32)
            fxb = Buf()
            pcb_ = Buf()
            pc = self.sb("poolc", [128, 64], F32)
            fw.dma(fw.sp, "a", pc[:], self.poolc_d, writes=[pcb_])
            for g in range(4):
                w = (2, 4, 8, 16)[g]
                wu, wub = self.wload(wpool, win[:, 5120 + g * 256:5120 + (g + 1) * 256])
                pwt, pwtb = pwpool.next()
                fw.dma(fw.pool, "w", pwt[:], self.pw_d[l, g].rearrange("(k p) n -> p k n", p=128), writes=[pwtb])
                for q in range(4):
                    qs = slice(q * 512, (q + 1) * 512)
                    hr = [hb[k][q] for k in range(KD)]
                    pl, plb = plpool.next()
                    us = []
                    for cc in range(2):
                        u, ub = upool.next()
                        if q == 0:
                            fw.op(fw.dve, lambda h: h.memset(u[:, 0:HL], 0.0), writes=[ub])
                        else:
                            fw.op(fw.dve, lambda h: h.tensor_copy(u[:, 0:HL], uhalo[:, cc, :]), reads=[uhb[cc]], writes=[ub])
                        pu, pub = self.psum()
                        fw.mm_group([(pu[:], wu[:, k, cc * 128:(cc + 1) * 128], hT[:, k, qs]) for k in range(KD)],
                                    reads=[wub] + hr, writes=[pub])
                        fw.op(fw.act, lambda h: h.copy(u[:, HL:HL + 512], pu[:]), reads=[pub], writes=[ub])
                        if q < 3:
                            fw.op(fw.dve, lambda h: h.tensor_copy(uhalo[:, cc, :], u[:, 512:512 + HL]), reads=[ub], writes=[uhb[cc]])
                        us.append((u, ub))
                    for cc in range(2):
                        u, ub = us[cc]
                        cur, curb = u, ub
                        d = 1
                        while d < w:
                            nx, nxb = spool.next()
                            fw.op(fw.dve, lambda h: h.tensor_tensor(nx[:, d:HL + 512], cur[:, d:HL + 512], cur[:, 0:HL + 512 - d], ALU.add),
                                  reads=[curb], writes=[nxb])
                            cur, curb = nx, nxb
                            d *= 2
                        fw.op(fw.dve, lambda h: h.scalar_tensor_tensor(pl[:, cc, :], cur[:, HL:HL + 512], 1.0 / w, u[:, HL:HL + 512],
                                                                       ALU.mult, ALU.subtract),
                              reads=[curb, ub], writes=[plb])
                        if q == 0:
                            fw.op(fw.dve, lambda h: h.tensor_tensor(fx[:], cur[:, HL:HL + 16], pc[:, g * 16:(g + 1) * 16], ALU.mult),
                                  reads=[curb, pcb_], writes=[fxb])
                            fw.op(fw.dve, lambda h: h.tensor_tensor(pl[:, cc, 0:16], fx[:], u[:, HL:HL + 16], ALU.subtract),
                                  reads=[fxb, ub, plb], writes=[plb])
                    for oc in range(2):
                        c = g * 2 + oc
                        py, pyb = self.psum()
                        fw.mm_group([(py[:], pwt[:, kc, oc * 128:(oc + 1) * 128], pl[:, kc, :]) for kc in range(2)],
                                    reads=[pwtb, plb], writes=[pyb])
                        fw.op(fw.act, lambda h: h.mul(y1[:, c, qs], py[:], self.vcol(l, V_PSC, c)), reads=[pyb, self.vecs_b], writes=[y1b[c][q]])
            self.gate_accum(l, self.po_d[l], y1, lambda k, q: y1b[k][q], 6144 + 2 * D, win, hT, hb, R_, rb, wpool, sgpool)

    def norm_full(self, l, n_norm, hT, hb):
        fw = self.fw
        for tb in range(4):
            ts = slice(tb * 512, (tb + 1) * 512)
            for c in range(KD):
                fw.op(fw.dve, lambda h: h.scalar_tensor_tensor(
                    hT[:, c, ts], self.x[:, c, ts], self.gcol(l, n_norm, c), self.rx[:, ts], ALU.mult, ALU.mult),
                    reads=[self.xb[c][tb], self.rxb[tb], self.vecs_b], writes=[hb[c][tb]])

    def proj_norm_res(self, l, w2d, inT, inb, n_norm, pre=None):
        fw = self.fw
        with self.scope():
            wpool = TPool(self, "wres", [128, KD, D], BF16, 1) if pre is None else None
            ypool = TPool(self, "yt", [128, KD, 512], F32, 3)
            rpool = TPool(self, "rstdt", [128, 512], F32, 2)
            sqpool = TPool(self, "sq", [128, 512], BF16, 5)
            rtpool = TPool(self, "rt", [128, 512], F32, 2)
            tmppool = TPool(self, "tmp", [128, 512], F32, 3)
            wt, wbs = self.wload_cols(wpool, w2d, 4) if pre is None else pre

            B05 = [0, 1, 2, 3, 4]

            def mm_stage(tb):
                ts = slice(tb * 512, (tb + 1) * 512)
                yt, ytb = ypool.next()
                sp_, spb_ = self.ps[5 + tb % 3], self.psb[5 + tb % 3]
                pend = []

                def stat_mm(mo_, sq_, sqb_):
                    fw.mm(sp_[:], self.onesD[:], sq_[:], mo_ == 0, mo_ == KD - 1, reads=[sqb_, self.cb], writes=[spb_])
                for mo in range(KD):
                    p, pb = self.psum(B05)
                    fw.mm_group([(p[:], wt[:, k, mo * 128:(mo + 1) * 128], inT[:, k, ts]) for k in range(KD)],
                                reads=[wbs[mo // 2]] + [inb[k][tb] for k in range(KD)], writes=[pb])
                    fw.op(fw.act, lambda h: h.copy(yt[:, mo, :], p[:]), reads=[pb], writes=[ytb])
                    sq, sqb = sqpool.next()
                    fw.op(fw.act, lambda h: h.activation(sq[:], p[:], AF.Square), reads=[pb], writes=[sqb])
                    pend.append((mo, sq, sqb))
                    if len(pend) > 2:
                        stat_mm(*pend.pop(0))
                while pend:
                    stat_mm(*pend.pop(0))
                return yt, ytb

            def epi1(tb, yt, ytb):
                rstd, rstd_b = rpool.next()
                sp_, spb_ = self.ps[5 + tb % 3], self.psb[5 + tb % 3]
                rt, rtb = rtpool.next()
                fw.op(fw.act, lambda h: h.activation(rt[:], sp_[:], AF.Ln, bias=self.epsr[:], scale=1.0),
                      reads=[spb_, self.cb], writes=[rtb])
                fw.op(fw.act, lambda h: h.activation(rstd[:], rt[:], AF.Exp, scale=-0.5), reads=[rtb], writes=[rstd_b])
                return rstd, rstd_b

            def epi2(tb, yt, ytb, rstd, rstd_b):
                ts = slice(tb * 512, (tb + 1) * 512)
                for c in range(KD):
                    tmp, tmpb = tmppool.next()
                    fw.op(fw.dve, lambda h: h.scalar_tensor_tensor(
                        tmp[:], yt[:, c, :], self.gcol(l, n_norm, c), rstd[:], ALU.mult, ALU.mult),
                        reads=[ytb, rstd_b, self.vecs_b], writes=[tmpb])
                    fw.op(fw.dve, lambda h: h.tensor_tensor(
                        self.x[:, c, ts], self.x[:, c, ts], tmp[:], ALU.add),
                        reads=[tmpb, self.xb[c][tb]], writes=[self.xb[c][tb]])

            ys = [mm_stage(0), mm_stage(1)]
            r0 = epi1(0, *ys[0])
            rs_ = {0: r0}
            for tb in range(4):
                if tb + 2 < 4:
                    ys.append(mm_stage(tb + 2))
                epi2(tb, *ys[tb], *rs_[tb])
                if tb + 1 < 4:
                    rs_[tb + 1] = epi1(tb + 1, *ys[tb + 1])
                self.update_rstd(tb, sqpool, rtpool, banks=B05)

    def ph_xattn(self, l):
        fw = self.fw
        xq = self.xq_d[l]
        xkv = self.xkv_d[l]
        with self.scope():
          oT = self.sb("oTx", [128, KD, S], BF16)
          ob = [[Buf() for _ in range(4)] for _ in range(KD)]
          xo_pool = TPool(self, "wresx", [128, KD, D], BF16, 1)
          with self.scope():
            hT = self.sb("hTx", [128, KD, S], BF16)
            hb = [[Buf() for _ in range(4)] for _ in range(KD)]
            KxT = self.sb("KxT", [128, KD, MEM], BF16)
            kxb = Buf()
            Vx = self.sb("Vx", [128, 2, D], BF16)
            vxb = Buf()
            with self.scope():
                memf = self.sb("memf", [128, KD, MEM], F32)
                memfb = Buf()
                memn = self.sb("memn", [128, KD, MEM], BF16)
                memnb = Buf()
                rstd = self.sb("rstdm", [128, MEM], F32)
                rstd_b = Buf()
                sqpool = TPool(self, "sq", [128, 512], BF16, 3)
                rtpool = TPool(self, "rt", [128, 512], F32, 2)
                wpool = TPool(self, "wkv", [128, KD, 512], BF16, 2)
                fw.dma(fw.sp, "a", memf[:], self.memT_d.rearrange("(c p) t -> p c t", p=128), writes=[memfb])
                srcs = [(lambda t_, c=c: memf[:, c, :], lambda t_: [memfb]) for c in range(KD)]
                self.rstd_blocks(srcs, 1, rstd, rstd_b, self.onesD[:], self.epsr[:], sqpool, rtpool, W=MEM)
                for c in range(KD):
                    fw.op(fw.dve, lambda h: h.scalar_tensor_tensor(
                        memn[:, c, :], memf[:, c, :], self.vcol(l, V_MN, c), rstd[:], ALU.mult, ALU.mult),
                        reads=[memfb, rstd_b, self.vecs_b], writes=[memnb])
                for cg in range(2):
                    wt, wb = self.wload(wpool, xkv[:, cg * 512:(cg + 1) * 512])
                    for mm_ in range(4):
                        mc = cg * 4 + mm_
                        p, pb = self.psum()
                        fw.mm_group([(p[:, :MEM], wt[:, k, mm_ * 128:(mm_ + 1) * 128], memn[:, k, :]) for k in range(KD)],
                                    reads=[wb, memnb], writes=[pb])
                        fw.op(fw.act, lambda h: h.copy(KxT[:, mc, :], p[:, :MEM]), reads=[pb], writes=[kxb])
                for cg in range(2):
                    wt, wb = self.wload(wpool, xkv[:, D + cg * 512:D + (cg + 1) * 512])
                    for mb in range(2):
                        p, pb = self.psum()
                        fw.mm_group([(p[:], memn[:, k, mb * 128:(mb + 1) * 128], wt[:, k, :]) for k in range(KD)],
                                    reads=[wb, memnb], writes=[pb])
                        fw.op(fw.act, lambda h: h.copy(Vx[:, mb, cg * 512:(cg + 1) * 512], p[:]), reads=[pb], writes=[vxb])
            self.norm_full(l, 4, hT, hb)
            with self.scope():
                wqpool = TPool(self, "wq", [128, KD, 256], BF16, 2)
                qpool = TPool(self, "qTh", [128, 2, S], BF16, 2)
                epool = TPool(self, "E", [128, 512], BF16, 6)
                rzpool = TPool(self, "rz", [128, 512], F32, 2)
                scale = 1.0 / 16.0
                def q_tasks(hd_, wq, wqb, qT, qb):
                    ts_ = []
                    for cc in range(2):
                        for tb in range(4):
                            def t_(cc=cc, tb=tb):
                                ts = slice(tb * 512, (tb + 1) * 512)
                                p, pb = self.psum()
                                fw.mm_group([(p[:], wq[:, k, cc * 128:(cc + 1) * 128], hT[:, k, ts]) for k in range(KD)],
                                            reads=[wqb] + [hb[k][tb] for k in range(KD)], writes=[pb])
                                fw.op(fw.act, lambda h: h.copy(qT[:, cc, ts], p[:]), reads=[pb], writes=[qb])
                            ts_.append(t_)
                    return ts_

                wq, wqb = self.wload(wqpool, xq[:, 0:256])
                qT, qb = qpool.next()
                for t_ in q_tasks(0, wq, wqb, qT, qb):
                    t_()
                xo_pre = self.wload_cols(xo_pool, self.xo_d[l], 4)
                for hd in range(XH):
                    tasks = []
                    nq = None
                    if hd + 1 < XH:
                        wqn, wqnb = self.wload(wqpool, xq[:, (hd + 1) * 256:(hd + 2) * 256])
                        nq = qpool.next()
                        tasks = q_tasks(hd + 1, wqn, wqnb, nq[0], nq[1])

                    def emit_s(tb):
                        ts = slice(tb * 512, (tb + 1) * 512)
                        es = []
                        for mb in range(2):
                            p, pb = self.psum()
                            fw.mm_group([(p[:], KxT[:, hd * 2 + cc, mb * 128:(mb + 1) * 128], qT[:, cc, ts]) for cc in range(2)],
                                        reads=[kxb, qb], writes=[pb])
                            e, eb = epool.next()
                            fw.op(fw.act, lambda h: h.activation(e[:], p[:], AF.Exp, scale=scale), reads=[pb], writes=[eb])
                            es.append((e, eb))
                        return es

                    nxt = emit_s(0)
                    for tb in range(4):
                        ts = slice(tb * 512, (tb + 1) * 512)
                        es = nxt
                        if tb < 3:
                            nxt = emit_s(tb + 1)
                        for _ in range(2):
                            if tasks:
                                tasks.pop(0)()
                        pz, pzb = self.psum()
                        fw.mm_group([(pz[:], self.ones1[:], es[mb][0][:]) for mb in range(2)],
                                    reads=[es[0][1], es[1][1], self.cb], writes=[pzb])
                        rz, rzb = rzpool.next()
                        self.recip(rz[:], pz[:], [pzb], [rzb])
                        for cc in range(2):
                            po, pob = self.psum()
                            fw.mm_group([(po[:], Vx[:, mb, hd * 256 + cc * 128:hd * 256 + (cc + 1) * 128], es[mb][0][:])
                                         for mb in range(2)], reads=[vxb, es[0][1], es[1][1]], writes=[pob])
                            fw.op(fw.dve, lambda h: h.tensor_tensor(oT[:, hd * 2 + cc, ts], po[:], rz[:], ALU.mult),
                                  reads=[pob, rzb], writes=[ob[hd * 2 + cc][tb]])
                    while tasks:
                        tasks.pop(0)()
                    if nq is not None:
                        qT, qb = nq
          self.proj_norm_res(l, self.xo_d[l], oT, ob, 5, pre=xo_pre)


def _t5_bucket_np(n):
    n = np.maximum(n, 0)
    max_exact = 16
    nf = np.maximum(n, 1).astype(np.float32)
    large = max_exact + (np.log(nf / max_exact) / math.log(128 / max_exact) * (32 - max_exact)).astype(np.int32)
    large = np.minimum(large, 31)
    return np.where(n < max_exact, n, large)


def _prep_shared(inp):
    f = np.float32
    sh = {}
    vecs = np.zeros((128, L * V_PER), f)
    for l in range(L):
        o = l * V_PER
        vecs[:, o + V_NG:o + V_NG + 64] = inp["norm_g"][l].reshape(8, 8, 128).transpose(2, 0, 1).reshape(128, 64)
        vecs[:, o + V_CDW:o + V_CDW + 8 * CK] = inp["conv_dw"][l].reshape(CK, 8, 128).transpose(2, 1, 0).reshape(128, 8 * CK)
        for off, key in ((V_CDB, "conv_dw_b"), (V_LNG, "conv_ln_g"), (V_LNB, "conv_ln_b"),
                         (V_PSC, "pool_scale"), (V_MN, "mem_norm")):
            vecs[:, o + off:o + off + 8] = inp[key][l].reshape(8, 128).T
        vecs[:, o + V_SUB] = inp["attn_subln"][l]
    sh["vecs"] = vecs
    sh["lamb"] = np.ascontiguousarray(np.broadcast_to(inp["attn_lam"].reshape(1, L * 256), (128, L * 256))).astype(f)
    k = np.arange(128)[:, None]
    j = np.arange(256)[None, :]
    dist = j - k
    idx = _t5_bucket_np(dist)
    g = inp["rel_bias"][idx]
    g = np.where((dist >= 0)[:, :, None], g, f(NEG))
    sh["bstrip"] = np.ascontiguousarray(g.transpose(0, 2, 1).reshape(128, NH * 256)).astype(f)
    sh["b31b"] = np.ascontiguousarray(np.broadcast_to(inp["rel_bias"][31][None, :], (128, NH))).astype(f)
    sh["ident"] = np.eye(128, dtype=f)
    pc = np.zeros((128, 64), f)
    for gi, w in enumerate((2, 4, 8, 16)):
        pc[:, gi * 16:(gi + 1) * 16] = 1.0 / np.minimum(np.arange(16) + 1, w)
    sh["poolc"] = pc
    for key in ("ffn_w1", "ffn_w3", "ffn_w2", "w_in", "conv_pw", "attn_o", "pool_w", "pool_o", "w_out",
                "xattn_q", "xattn_kv", "xattn_o"):
        sh[key] = np.ascontiguousarray(inp[key], dtype=f)
    return sh


_CACHE = {}


def _get_nc(depth, plan=None):
    key = (depth, tuple(plan) if plan else None)
    if key not in _CACHE:
        nc = bass.Bass("TRN2", target_bir_lowering=False)
        kb = KB(nc, depth, plan)
        kb.build()
        _CACHE[key] = (nc, kb)
    return _CACHE[key]


def run(inputs, depth=L, plan=None, cores=8, trace=False):
    inp = {k: np.asarray(v) for k, v in inputs.items()}
    nc, kb = _get_nc(depth, plan)
    sh = _prep_shared(inp)
    in_maps = []
    for b in range(cores):
        m = dict(sh)
        m["xT"] = np.ascontiguousarray(inp["x"][b].T, dtype=np.float32)
        m["memT"] = np.ascontiguousarray(inp["mem"][b].T, dtype=np.float32)
        in_maps.append(m)
    res = run_bass_kernel_spmd(nc, in_maps, core_ids=list(range(cores)), trace=trace)
    out = np.stack([np.ascontiguousarray(r["outT"].T) for r in res.results], axis=0)
    return out, res, kb


def kernel(**inputs):
    out, _, _ = run(inputs)
    return out.astype(np.float32)
```
